# Optimizing a Trainium2 kernel written in Bass

```python
import math
import jax, jax.numpy as jnp
from jax import lax
import numpy as np

D_MODEL = 1024
BATCH = 8
SEQ = 4096
DEPTH = 4

CHUNK = 64
Q_BLOCK = 128
MIX_WIDTH = D_MODEL
ATT_WIDTH = MIX_WIDTH // 2
REC_WIDTH = MIX_WIDTH - ATT_WIDTH
H_A = 4
D_A = ATT_WIDTH // (2 * H_A)
H_R = 4
D_K = REC_WIDTH // H_R
D_V = REC_WIDTH // H_R
D_FF = 4 * D_MODEL
NUM_BUCKETS = 32
MAX_DISTANCE = 128
IN_COLS = 3 * ATT_WIDTH + 4 * REC_WIDTH
EPS = 1e-6
NEG_INF = -1e30

kernel_name = "hymba_diffattn_hgrn2_trunk"


def rms_norm(x, g):
    xf = x.astype(jnp.float32)
    y = xf * lax.rsqrt(jnp.mean(xf * xf, axis=-1, keepdims=True) + EPS)
    return (y * g.astype(jnp.float32)).astype(x.dtype)


def t5_bucket(rel):
    n_half = NUM_BUCKETS // 2
    max_exact = n_half // 2
    ret = jnp.where(rel > 0, n_half, 0)
    n = jnp.abs(rel)
    nf = jnp.maximum(n, 1).astype(jnp.float32)
    large = max_exact + (jnp.log(nf / max_exact) / math.log(MAX_DISTANCE / max_exact)
                         * (n_half - max_exact)).astype(jnp.int32)
    large = jnp.minimum(large, n_half - 1)
    return ret + jnp.where(n < max_exact, n, large)


def diff_attention(q1, q2, k1, k2, v, lam, rel_bias):
    B, H, S, d = q1.shape
    nblk = S // Q_BLOCK
    scale = d ** -0.5
    kpos = jnp.arange(S)

    def block(args):
        i, qa, qb = args
        qpos = i * Q_BLOCK + jnp.arange(Q_BLOCK)
        allowed = (kpos[None, :] // CHUNK) <= (qpos[:, None] // CHUNK)
        bias = rel_bias[t5_bucket(kpos[None, :] - qpos[:, None])]
        bias = jnp.transpose(bias, (2, 0, 1)).astype(jnp.float32)

        def probs(qx, kx):
            s = jnp.einsum('bhqd,bhkd->bhqk', qx, kx).astype(jnp.float32) * scale + bias
            s = jnp.where(allowed, s, NEG_INF)
            return jax.nn.softmax(s, axis=-1)

        p = probs(qa, k1) - lam * probs(qb, k2)
        return jnp.einsum('bhqk,bhkv->bhqv', p.astype(v.dtype), v)

    to_blocks = lambda t: t.reshape(B, H, nblk, Q_BLOCK, d).transpose(2, 0, 1, 3, 4)
    out = lax.map(block, (jnp.arange(nblk), to_blocks(q1), to_blocks(q2)))
    return out.transpose(1, 2, 0, 3, 4).reshape(B, H, S, v.shape[-1])


def hgrn2_chunked(q, g, k, v):
    B, H, S, dk = q.shape
    dv = v.shape[-1]
    nC = S // CHUNK
    to_chunks = lambda t: t.reshape(B, H, nC, CHUNK, t.shape[-1]).transpose(2, 0, 1, 3, 4)
    tri = jnp.tril(jnp.ones((CHUNK, CHUNK), dtype=bool))

    def step(state, inp):
        qc, gc, kc, vc = inp
        bc = jnp.cumsum(gc, axis=-2)
        inter = jnp.einsum('bhtk,bhkv->bhtv', qc * jnp.exp(bc), state)
        diff = bc[:, :, :, None, :] - bc[:, :, None, :, :]
        decay = jnp.exp(jnp.where(tri[:, :, None], diff, -jnp.inf))
        attn = jnp.einsum('bhtk,bhtsk,bhsk->bhts', qc, decay, kc)
        intra = jnp.einsum('bhts,bhsv->bhtv', attn, vc)
        blast = bc[:, :, -1:, :]
        state = (jnp.exp(blast[:, :, 0, :])[..., None] * state
                 + jnp.einsum('bhsk,bhsv->bhkv', kc * jnp.exp(blast - bc), vc))
        return state, inter + intra

    state0 = jnp.zeros((B, H, dk, dv), jnp.float32)
    _, out = lax.scan(step, state0, (to_chunks(q), to_chunks(g), to_chunks(k), to_chunks(v)))
    return out.transpose(1, 2, 0, 3, 4).reshape(B, H, S, dv)


def setup_inputs(seed: int = 0) -> dict:
    key = jax.random.key(seed)
    ks = jax.random.split(key, 16)
    nrm = lambda k, shape, s: jax.random.normal(k, shape, jnp.float32) * s
    return {
        "x": nrm(ks[0], (BATCH, SEQ, D_MODEL), 1.0),
        "norm1_g": 1.0 + nrm(ks[1], (DEPTH, D_MODEL), 0.02),
        "w_in": nrm(ks[2], (DEPTH, D_MODEL, IN_COLS), D_MODEL ** -0.5),
        "lam_qk": nrm(ks[3], (DEPTH, 4, D_A), 0.1),
        "attn_norm_g": 1.0 + nrm(ks[4], (DEPTH, 2 * D_A), 0.02),
        "lb_logits": nrm(ks[5], (DEPTH, REC_WIDTH), 0.1),
        "hgrn_norm_g": 1.0 + nrm(ks[6], (DEPTH, D_V), 0.02),
        "w_out": nrm(ks[7], (DEPTH, MIX_WIDTH, D_MODEL), MIX_WIDTH ** -0.5),
        "norm2_g": 1.0 + nrm(ks[8], (DEPTH, D_MODEL), 0.02),
        "w_up": nrm(ks[9], (DEPTH, D_MODEL, D_FF), D_MODEL ** -0.5),
        "w_down": nrm(ks[10], (DEPTH, D_FF, D_MODEL), D_FF ** -0.5),
        "rel_bias": nrm(ks[11], (NUM_BUCKETS, H_A), 0.5),
        "final_g": 1.0 + nrm(ks[12], (D_MODEL,), 0.02),
    }


def reference(x, norm1_g, w_in, lam_qk, attn_norm_g, lb_logits, hgrn_norm_g,
              w_out, norm2_g, w_up, w_down, rel_bias, final_g):
    B, S, _ = x.shape
    lb_all = jnp.cumsum(jax.nn.softmax(lb_logits.astype(jnp.float32), axis=0), axis=0)
    lb_all = lb_all - lb_all[0:1]

    for l in range(DEPTH):
        h = rms_norm(x, norm1_g[l])
        z = jnp.einsum('bsd,dc->bsc', h, w_in[l])
        aq, ak, av, rq, rf, ri, rg = jnp.split(
            z, np.cumsum([ATT_WIDTH] * 3 + [REC_WIDTH] * 3).tolist(), axis=-1)

        qh = aq.reshape(B, S, H_A, 2, D_A).transpose(0, 2, 1, 3, 4)
        kh = ak.reshape(B, S, H_A, 2, D_A).transpose(0, 2, 1, 3, 4)
        vh = av.reshape(B, S, H_A, 2 * D_A).transpose(0, 2, 1, 3)
        lam_init = 0.8 - 0.6 * math.exp(-0.3 * l)
        lq = lam_qk[l].astype(jnp.float32)
        lam = jnp.exp(jnp.sum(lq[0] * lq[1])) - jnp.exp(jnp.sum(lq[2] * lq[3])) + lam_init
        oa = diff_attention(qh[..., 0, :], qh[..., 1, :], kh[..., 0, :], kh[..., 1, :],
                            vh, lam, rel_bias)
        oa = rms_norm(oa, attn_norm_g[l]) * (1.0 - lam_init)
        oa = oa.transpose(0, 2, 1, 3).reshape(B, S, ATT_WIDTH)

        lb = lb_all[l].reshape(H_R, D_K)[None, :, None, :]
        to_heads = lambda t, d: t.reshape(B, S, H_R, d).transpose(0, 2, 1, 3).astype(jnp.float32)
        rf_h = to_heads(rf, D_K)
        log_f = jnp.logaddexp(jnp.log(lb), jnp.log1p(-lb) + jax.nn.log_sigmoid(rf_h))
        k_in = -jnp.expm1(log_f)
        q_r = jax.nn.silu(to_heads(rq, D_K))
        orr = hgrn2_chunked(q_r, log_f, k_in, to_heads(ri, D_V))
        orr = rms_norm(orr.transpose(0, 2, 1, 3), hgrn_norm_g[l]).reshape(B, S, REC_WIDTH)
        orr = orr.astype(x.dtype) * jax.nn.silu(rg)

        mixed = jnp.concatenate([oa.astype(x.dtype), orr], axis=-1)
        x = x + jnp.einsum('bsc,cd->bsd', mixed, w_out[l])

        h2 = rms_norm(x, norm2_g[l])
        u = jax.nn.relu(jnp.einsum('bsd,df->bsf', h2, w_up[l]))
        x = x + jnp.einsum('bsf,fd->bsd', u * u, w_down[l])

    return rms_norm(x, final_g)
```

```python
import math
from contextlib import ExitStack

import numpy as np
import ml_dtypes
import concourse.bass as bass
import concourse.mybir as mybir
from concourse.bass_utils import run_bass_kernel_spmd

F32 = mybir.dt.float32
BF16 = mybir.dt.bfloat16
AF = mybir.ActivationFunctionType
ALU = mybir.AluOpType
AX = mybir.AxisListType

S = 4096
D = 1024
DEPTH = 4
DFF = 4096
INC = 3584
EPS = 1e-6
NCORES = 8


class Sem:
    def __init__(self, nc, name):
        self.h = nc.alloc_semaphore(name)
        self.cnt = 0
        self.name = name


class Eng:
    def __init__(self, k, h, name):
        self.k = k
        self.h = h
        self.sem = Sem(k.nc, "e_" + name)
        self.seen = {}

    def wait(self, *toks):
        for t in toks:
            if t is None:
                continue
            if isinstance(t, list):
                self.wait(*t)
                continue
            sem, val = t
            if self.seen.get(sem.name, 0) >= val:
                continue
            self.h.wait_ge(sem.h, val)
            self.seen[sem.name] = val

    def mark(self, inst):
        self.sem.cnt += 1
        inst.then_inc(self.sem.h, 1)
        tok = (self.sem, self.sem.cnt)
        self.k.latest[self.sem.name] = tok
        return tok

    def dma(self, out, in_, sem):
        inst = self.h.dma_start(out=out, in_=in_)
        sem.cnt += 16
        inst.then_inc(sem.h, 16)
        tok = (sem, sem.cnt)
        self.k.latest[sem.name] = tok
        return tok


class Rot:
    def __init__(self, items):
        self.items = items
        self.i = 0
        self.free = [[] for _ in items]

    def get(self):
        idx = self.i % len(self.items)
        self.i += 1
        return idx, self.items[idx], self.free[idx]

    def release(self, idx, *toks):
        self.free[idx] = list(toks)


class K:
    def __init__(self, nc):
        self.nc = nc
        self.latest = {}
        self.pe = Eng(self, nc.tensor, "pe")
        self.act = Eng(self, nc.scalar, "act")
        self.dve = Eng(self, nc.vector, "dve")
        self.pool = Eng(self, nc.gpsimd, "pool")
        self.sp = Eng(self, nc.sync, "sp")
        self.engs = [self.pe, self.act, self.dve, self.pool, self.sp]
        self.sems = {}
        self.ps = [nc.alloc_psum_tensor(f"psb{i}", [128, 512], F32) for i in range(8)]

    def sem(self, name):
        if name not in self.sems:
            self.sems[name] = Sem(self.nc, name)
        return self.sems[name]

    def barrier(self):
        toks = list(self.latest.values())
        for e in self.engs:
            e.wait(*toks)


def bcast_mid(ap2d, n):
    p, a = ap2d.shape
    return ap2d.unsqueeze(2).broadcast_to([p, a, n])


def phase_D(k, l, dr, final):
    nc = k.nc
    pe, act, dve, pool, sp = k.pe, k.act, k.dve, k.pool, k.sp
    TB = 256
    NB = S // TB
    xin = dr["x_in"]
    xout = dr["x_out"]
    mixT = dr["mixT"]
    with ExitStack() as es:
        def sb(name, shape, dt):
            return es.enter_context(nc.sbuf_tensor(f"{name}_L{l}", shape, dt))
        w_out_sb = sb("w_out_sb", [128, 8, 1024], BF16)
        w_up_sb = sb("w_up_sb", [128, 8, 4096], BF16)
        w_dn_sb = sb("w_dn_sb", [128, 32, 1024], BF16)
        mT = [sb("mT0", [128, 8, TB], BF16)]
        xs = [sb(f"xs{i}", [128, 2, 1024], F32) for i in range(2)]
        h2 = sb("h2", [128, 2, 1024], BF16)
        h2T = sb("h2T", [128, 8, TB], BF16)
        u2T = sb("u2T", [128, 32, TB], BF16)
        sq = [sb(f"sq{i}", [128, TB], F32) for i in range(2)]
        ss = sb("ssD", [128, 4], F32)
        rs = sb("rsD", [128, 4], F32)
        rstd = sb("rstdD", [128, 4], F32)
        if final:
            gF = sb("gF_sb", [128, 1024], F32)
            tok_gF = sp.dma(gF[:, :], dr["final_g"].partition_broadcast(128), k.sem("consts"))

        wtok_out = pool.dma(w_out_sb[:, :, :], dr["w_out"][l].rearrange("(c p) d -> p c d", p=128), k.sem("w_out"))
        wtok_up = []
        for dc in range(8):
            wtok_up.append(pool.dma(w_up_sb[:, dc, :], dr["w_up"][l, dc * 128:(dc + 1) * 128, :], k.sem(f"w_up{dc % 4}")))
        wtok_dn = []
        for j in range(4):
            wtok_dn.append(pool.dma(w_dn_sb[:, j * 8:(j + 1) * 8, :],
                                    dr["w_down"][l, j * 1024:(j + 1) * 1024, :].rearrange("(c p) d -> p c d", p=128),
                                    k.sem(f"w_dn{j}")))

        mT_sem = [k.sem("D_mT0")]
        xs_sem = [k.sem(f"D_xs{i}") for i in range(2)]
        st_sem = [k.sem(f"D_st{i}") for i in range(2)]
        mT_free = [[]]
        xs_free = [[], []]
        ld_tok = {}
        ldm_tok = {}

        def issue_mT(b):
            t0 = b * TB
            sp.wait(mT_free[0])
            ldm_tok[b] = sp.dma(mT[0][:, :, :], mixT.rearrange("(c p) t -> p c t", p=128)[:, :, t0:t0 + TB], mT_sem[0])

        def issue_loads(b):
            s = b % 2
            t0 = b * TB
            sp.wait(xs_free[s])
            ld_tok[b] = sp.dma(xs[s][:, :, :], xin[t0:t0 + TB, :].rearrange("(t p) d -> p t d", p=128), xs_sem[s])

        accA = Rot([k.ps[0], k.ps[1]])
        accU = Rot([k.ps[2], k.ps[3]])
        accT = Rot([k.ps[4], k.ps[5]])
        sqr = Rot(sq)
        h2_free = []
        h2T_free = []
        u2T_free = []
        g2 = dr["g2_sb"][:, l * 8:(l + 1) * 8]

        issue_loads(0)
        issue_mT(0)
        for b in range(NB):
            s = b % 2
            t0 = b * TB
            if b + 1 < NB:
                issue_loads(b + 1)
            tok_mT, tok_xs = ldm_tok[b], ld_tok[b]
            x1 = [[None, None], [None, None]]
            mm_reads = []
            for tt in range(2):
                for hf in range(2):
                    bi, bank, fr = accA.get()
                    pe.wait(tok_mT, wtok_out, fr)
                    for c in range(8):
                        ins = pe.h.matmul(bank[:, :], lhsT=mT[0][:, c, tt * 128:(tt + 1) * 128],
                                          rhs=w_out_sb[:, c, hf * 512:(hf + 1) * 512], start=(c == 0), stop=(c == 7))
                    tmm = pe.mark(ins)
                    mm_reads.append(tmm)
                    dve.wait(tmm, tok_xs)
                    t = dve.mark(dve.h.tensor_tensor(out=xs[s][:, tt, hf * 512:(hf + 1) * 512], in0=bank[:, :],
                                                     in1=xs[s][:, tt, hf * 512:(hf + 1) * 512], op=ALU.add))
                    accA.release(bi, t)
                    x1[tt][hf] = t
            mT_free[0] = mm_reads
            if b + 1 < NB:
                issue_mT(b + 1)
            tok_h2 = []
            for tt in range(2):
                act.wait(x1[tt][0], x1[tt][1], h2_free)
                t = act.mark(act.h.activation(out=h2[:, tt, :], in_=xs[s][:, tt, :], func=AF.Square,
                                              accum_out=ss[:, tt:tt + 1]))
                act.wait(t)
                t = act.mark(act.h.activation(out=rs[:, tt:tt + 1], in_=ss[:, tt:tt + 1], func=AF.Sqrt,
                                              scale=1.0 / D, bias=dr["eps_sb"][:, 0:1]))
                dve.wait(t)
                t = dve.mark(dve.h.reciprocal(out=rstd[:, tt:tt + 1], in_=rs[:, tt:tt + 1]))
                act.wait(t, h2_free)
                t = act.mark(act.h.activation(out=h2[:, tt, :], in_=xs[s][:, tt, :], func=AF.Copy,
                                              scale=rstd[:, tt:tt + 1]))
                tok_h2.append(t)
            tok_h2T = []
            h2_reads = []
            for tt in range(2):
                bi, bank, fr = accT.get()
                bv = bank[:, :].bitcast(BF16)
                pe.wait(tok_h2[tt], fr)
                for c in range(8):
                    ins = pe.h.transpose(out=bv[:, c * 128:(c + 1) * 128], in_=h2[:, tt, c * 128:(c + 1) * 128],
                                         identity=dr["ident"][:, :])
                tmm = pe.mark(ins)
                h2_reads.append(tmm)
                dve.wait(tmm, h2T_free)
                t = dve.mark(dve.h.tensor_tensor(out=h2T[:, :, tt * 128:(tt + 1) * 128],
                                                 in0=bv.rearrange("p (c t) -> p c t", c=8),
                                                 in1=bcast_mid(g2, 128), op=ALU.mult))
                accT.release(bi, t)
                tok_h2T.append(t)
            h2_free = h2_reads
            tok_u = []
            up_reads = []
            for fc in range(32):
                bi, bank, fr = accU.get()
                pe.wait(tok_h2T, wtok_up, fr)
                for c in range(8):
                    ins = pe.h.matmul(bank[:, 0:TB], lhsT=w_up_sb[:, c, fc * 128:(fc + 1) * 128],
                                      rhs=h2T[:, c, :], start=(c == 0), stop=(c == 7))
                tmm = pe.mark(ins)
                up_reads.append(tmm)
                si, sqt, sfr = sqr.get()
                act.wait(tmm, sfr)
                ta = act.mark(act.h.activation(out=sqt[:, :], in_=bank[:, 0:TB], func=AF.Square))
                dve.wait(ta, u2T_free)
                t = dve.mark(dve.h.scalar_tensor_tensor(out=u2T[:, fc, :], in0=bank[:, 0:TB], scalar=0.0,
                                                        in1=sqt[:, :], op0=ALU.is_gt, op1=ALU.mult))
                accU.release(bi, t)
                sqr.release(si, t)
                tok_u.append(t)
            h2T_free = [up_reads[-1]]
            x2 = []
            dn_reads = []
            for tt in range(2):
                for hf in range(2):
                    bi, bank, fr = accA.get()
                    pe.wait(tok_u[-1], wtok_dn, fr)
                    for c in range(32):
                        ins = pe.h.matmul(bank[:, :], lhsT=u2T[:, c, tt * 128:(tt + 1) * 128],
                                          rhs=w_dn_sb[:, c, hf * 512:(hf + 1) * 512], start=(c == 0), stop=(c == 31))
                    tmm = pe.mark(ins)
                    dn_reads.append(tmm)
                    dve.wait(tmm)
                    t = dve.mark(dve.h.tensor_tensor(out=xs[s][:, tt, hf * 512:(hf + 1) * 512], in0=bank[:, :],
                                                     in1=xs[s][:, tt, hf * 512:(hf + 1) * 512], op=ALU.add))
                    accA.release(bi, t)
                    x2.append(t)
            u2T_free = [dn_reads[-1]]
            if not final:
                sp.wait(x2)
                t = sp.dma(xout[t0:t0 + TB, :].rearrange("(t p) d -> p t d", p=128), xs[s][:, :, :], st_sem[s])
                xs_free[s] = [t]
            else:
                fin = []
                for tt in range(2):
                    act.wait(x2, h2_free)
                    t = act.mark(act.h.activation(out=h2[:, tt, :], in_=xs[s][:, tt, :], func=AF.Square,
                                                  accum_out=ss[:, 2 + tt:3 + tt]))
                    act.wait(t)
                    t = act.mark(act.h.activation(out=rs[:, 2 + tt:3 + tt], in_=ss[:, 2 + tt:3 + tt], func=AF.Sqrt,
                                                  scale=1.0 / D, bias=dr["eps_sb"][:, 0:1]))
                    dve.wait(t)
                    t = dve.mark(dve.h.reciprocal(out=rstd[:, 2 + tt:3 + tt], in_=rs[:, 2 + tt:3 + tt]))
                    dve.wait(t, tok_gF)
                    t = dve.mark(dve.h.scalar_tensor_tensor(out=xs[s][:, tt, :], in0=xs[s][:, tt, :],
                                                            scalar=rstd[:, 2 + tt:3 + tt], in1=gF[:, :],
                                                            op0=ALU.mult, op1=ALU.mult))
                    fin.append(t)
                sp.wait(fin)
                t = sp.dma(dr["y"][t0:t0 + TB, :].rearrange("(t p) d -> p t d", p=128), xs[s][:, :, :], st_sem[s])
                xs_free[s] = [t]
        k.barrier()


def phase_A(k, l, dr):
    nc = k.nc
    pe, act, dve, pool, sp = k.pe, k.act, k.dve, k.pool, k.sp
    TB = 512
    NB = S // TB
    xin = dr["x_in"]
    with ExitStack() as es:
        def sb(name, shape, dt):
            return es.enter_context(nc.sbuf_tensor(f"{name}_L{l}", shape, dt))
        w_in_sb = sb("w_in_sb", [128, 8, INC], BF16)
        xs = [sb(f"xa{i}", [128, 4, 1024], F32) for i in range(2)]
        junk = sb("junkA", [128, 1024], BF16)
        hs = sb("hsA", [128, 4, 1024], BF16)
        hT = sb("hTA", [128, 8, TB], BF16)
        ss = sb("ssA", [128, 4], F32)
        rs = sb("rsA", [128, 4], F32)
        rstd = sb("rstdA", [128, 4], F32)
        NST = 8
        stg = [sb(f"stgA{i}", [128, 512], BF16) for i in range(NST)]
        stgf = [sb(f"stgAf{i}", [128, 512], F32) for i in range(2)]
        sig = [sb(f"sigA{i}", [128, 512], F32) for i in range(2)]

        wtok = []
        for dc in range(8):
            wtok.append(pool.dma(w_in_sb[:, dc, :], dr["w_in"][l, dc * 128:(dc + 1) * 128, :], k.sem(f"w_in{dc % 4}")))

        xs_sem = [k.sem(f"A_xs{i}") for i in range(2)]
        xs_free = [[], []]
        ld_tok = {}

        def issue_loads(b):
            s = b % 2
            t0 = b * TB
            sp.wait(xs_free[s])
            ld_tok[b] = sp.dma(xs[s][:, :, :], xin[t0:t0 + TB, :].rearrange("(t p) d -> p t d", p=128), xs_sem[s])

        accT = Rot([k.ps[0], k.ps[1]])
        accM = Rot([k.ps[2], k.ps[3], k.ps[4], k.ps[5]])
        stg_r = Rot(stg)
        stg_sems = [k.sem(f"A_st{i}") for i in range(NST)]
        stgf_r = Rot(stgf)
        stgf_sems = [k.sem(f"A_stf{i}") for i in range(2)]
        sig_r = Rot(sig)
        hs_free = []
        hT_free = []
        g1 = dr["g1_sb"][:, l * 8:(l + 1) * 8]
        lbt = sb("lbA", [128, 512], F32)
        omlt = sb("omlA", [128, 512], F32)
        sp.dma(lbt[:, :], dr["lbrep"][:, l, :], k.sem("consts"))
        tok_lb = sp.dma(omlt[:, :], dr["omlrep"][:, l, :], k.sem("consts"))
        lb, oml = lbt[:, :], omlt[:, :]

        def store_bf(eng_tok_fn, dst):
            si, st, fr = stg_r.get()
            t = eng_tok_fn(st, fr)
            sp.wait(t)
            td = sp.dma(dst, st[:, :], stg_sems[si])
            stg_r.release(si, td)

        issue_loads(0)
        for b in range(NB):
            s = b % 2
            t0 = b * TB
            if b + 1 < NB:
                issue_loads(b + 1)
            tok_x = ld_tok[b]
            tok_hs = []
            for tt in range(4):
                act.wait(tok_x)
                t = act.mark(act.h.activation(out=junk[:, :], in_=xs[s][:, tt, :], func=AF.Square,
                                              accum_out=ss[:, tt:tt + 1]))
                act.wait(t)
                t = act.mark(act.h.activation(out=rs[:, tt:tt + 1], in_=ss[:, tt:tt + 1], func=AF.Sqrt,
                                              scale=1.0 / D, bias=dr["eps_sb"][:, 0:1]))
                dve.wait(t)
                t = dve.mark(dve.h.reciprocal(out=rstd[:, tt:tt + 1], in_=rs[:, tt:tt + 1]))
                act.wait(t, hs_free)
                t = act.mark(act.h.activation(out=hs[:, tt, :], in_=xs[s][:, tt, :], func=AF.Copy,
                                              scale=rstd[:, tt:tt + 1]))
                tok_hs.append(t)
            xs_free[s] = [tok_hs[-1]]
            tok_hT = []
            hs_reads = []
            for tt in range(4):
                bi, bank, fr = accT.get()
                bv = bank[:, :].bitcast(BF16)
                pe.wait(tok_hs[tt], fr)
                for c in range(8):
                    ins = pe.h.transpose(out=bv[:, c * 128:(c + 1) * 128], in_=hs[:, tt, c * 128:(c + 1) * 128],
                                         identity=dr["ident"][:, :])
                tmm = pe.mark(ins)
                hs_reads.append(tmm)
                dve.wait(tmm, hT_free)
                t = dve.mark(dve.h.tensor_tensor(out=hT[:, :, tt * 128:(tt + 1) * 128],
                                                 in0=bv.rearrange("p (c t) -> p c t", c=8),
                                                 in1=bcast_mid(g1, 128), op=ALU.mult))
                accT.release(bi, t)
                tok_hT.append(t)
            hs_free = hs_reads
            last_mm = None
            fm = [("qT", 0, False), ("kT", 512, False), ("rqT", 1536, True), ("rgT", 3072, True)]
            for name, cbase, silu in fm:
                for h in range(4):
                    bi, bank, fr = accM.get()
                    pe.wait(tok_hT, wtok, fr)
                    c0 = cbase + h * 128
                    for c in range(8):
                        ins = pe.h.matmul(bank[:, :], lhsT=w_in_sb[:, c, c0:c0 + 128], rhs=hT[:, c, :],
                                          start=(c == 0), stop=(c == 7))
                    tmm = pe.mark(ins)
                    last_mm = tmm

                    def fill(st, fr2, bank=bank, tmm=tmm, silu=silu, bi=bi):
                        if silu:
                            act.wait(tmm, fr2)
                            t = act.mark(act.h.activation(out=st[:, :], in_=bank[:, :], func=AF.Silu))
                        else:
                            dve.wait(tmm, fr2)
                            t = dve.mark(dve.h.tensor_copy(out=st[:, :], in_=bank[:, :]))
                        accM.release(bi, t)
                        return t
                    store_bf(fill, dr[name][h, :, t0:t0 + TB])
            for tt in range(4):
                r0 = t0 + tt * 128
                for name, cbase in (("v", 1024), ("ri", 2560)):
                    bi, bank, fr = accM.get()
                    pe.wait(tok_hT, wtok, fr)
                    for c in range(8):
                        ins = pe.h.matmul(bank[:, :], lhsT=hT[:, c, tt * 128:(tt + 1) * 128],
                                          rhs=w_in_sb[:, c, cbase:cbase + 512], start=(c == 0), stop=(c == 7))
                    tmm = pe.mark(ins)
                    last_mm = tmm

                    def fill(st, fr2, bank=bank, tmm=tmm, bi=bi):
                        dve.wait(tmm, fr2)
                        t = dve.mark(dve.h.tensor_copy(out=st[:, :], in_=bank[:, :]))
                        accM.release(bi, t)
                        return t
                    store_bf(fill, dr[name][r0:r0 + 128, :])
                bi, bank, fr = accM.get()
                pe.wait(tok_hT, wtok, fr)
                for c in range(8):
                    ins = pe.h.matmul(bank[:, :], lhsT=hT[:, c, tt * 128:(tt + 1) * 128],
                                      rhs=w_in_sb[:, c, 2048:2560], start=(c == 0), stop=(c == 7))
                tmm = pe.mark(ins)
                last_mm = tmm
                gi, sg, gfr = sig_r.get()
                act.wait(tmm, gfr)
                t = act.mark(act.h.activation(out=sg[:, :], in_=bank[:, :], func=AF.Sigmoid))
                accM.release(bi, t)
                dve.wait(t, tok_lb)
                t = dve.mark(dve.h.tensor_tensor(out=sg[:, :], in0=sg[:, :], in1=oml, op=ALU.mult))
                dve.wait(t)
                tf = dve.mark(dve.h.tensor_tensor(out=sg[:, :], in0=sg[:, :], in1=lb, op=ALU.add))
                def fillk(st, fr2, sg=sg, tf=tf):
                    dve.wait(tf, fr2)
                    return dve.mark(dve.h.tensor_scalar(out=st[:, :], in0=sg[:, :], scalar1=-1.0, scalar2=1.0,
                                                        op0=ALU.mult, op1=ALU.add))
                si, st, sfr = stg_r.get()
                tk = fillk(st, sfr)
                sp.wait(tk)
                td = sp.dma(dr["kk"][r0:r0 + 128, :], st[:, :], stg_sems[si])
                stg_r.release(si, td)
                fi, sf, ffr = stgf_r.get()
                act.wait(tf, ffr)
                tg = act.mark(act.h.activation(out=sf[:, :], in_=sg[:, :], func=AF.Ln))
                sig_r.release(gi, tg, tk)
                sp.wait(tg)
                td = sp.dma(dr["g"][r0:r0 + 128, :], sf[:, :], stgf_sems[fi])
                stgf_r.release(fi, td)
            hT_free = [last_mm]
        k.barrier()


def phase_B(k, l, dr):
    nc = k.nc
    pe, act, dve, pool, sp = k.pe, k.act, k.dve, k.pool, k.sp
    QC = 512
    NQ = S // QC
    qT, kT, v, mixT = dr["qT"], dr["kT"], dr["v"], dr["mixT"]
    with ExitStack() as es:
        def sb(name, shape, dt):
            return es.enter_context(nc.sbuf_tensor(f"{name}_L{l}", shape, dt))
        qp1 = [sb(f"qp1_{i}", [128, S], BF16) for i in range(2)]
        qp2 = [sb(f"qp2_{i}", [128, S], BF16) for i in range(2)]
        kTs = [sb(f"kTs{i}", [128, S], BF16) for i in range(2)]
        Vs = [sb(f"Vs{i}", [128, 32, 128], BF16) for i in range(2)]
        P = [[sb(f"P{i}_{m}", [128, QC], BF16) for m in range(2)] for i in range(2)]
        r1 = sb("Br1", [128, QC], F32)
        r2 = sb("Br2", [128, QC], F32)
        ta = sb("Bta", [128, QC], F32)
        tb = sb("Btb", [128, QC], F32)
        to = sb("Bto", [128, QC], F32)
        tsq = sb("Btsq", [128, QC], F32)
        trs = sb("Btrs", [128, QC], F32)
        ystg = [sb(f"Bys{i}", [128, QC], BF16) for i in range(2)]

        zt = []
        for i in range(2):
            zt.append(pool.mark(pool.h.memset(qp1[i][64:128, :], 0.0)))
            zt.append(pool.mark(pool.h.memset(qp2[i][0:64, :], 0.0)))

        hd_sem = [k.sem(f"B_hd{i}") for i in range(2)]
        hd_free = [[], []]
        hd_tok = {}

        def issue_head_loads(h):
            s = h % 2
            sp.wait(hd_free[s])
            sp.dma(kTs[s][:, :], kT[h, :, :], hd_sem[s])
            sp.dma(qp1[s][0:64, :], qT[h, 0:64, :], hd_sem[s])
            sp.dma(qp2[s][64:128, :], qT[h, 64:128, :], hd_sem[s])
            hd_tok[h] = sp.dma(Vs[s][:, :, :], v.rearrange("(t p) c -> p t c", p=128)[:, :, h * 128:(h + 1) * 128],
                               hd_sem[s])

        accS = Rot([(k.ps[4], k.ps[5]), (k.ps[6], k.ps[7])])
        O1, O2, s1, s2 = k.ps[0], k.ps[1], k.ps[2], k.ps[3]
        Prot = Rot(P)
        ysr = Rot(ystg)
        ys_sems = [k.sem(f"B_ys{i}") for i in range(2)]
        ident = dr["ident"]
        ones_bf = dr["ones_bf"]
        ones_f = dr["ones_f"]
        state = {"O_free": [], "fin_free": []}
        deferred = []

        def emit_S(h, qc, kt):
            s = h % 2
            q0 = qc * QC
            c0 = max(0, kt * 128 - q0)
            bi, (S1, S2), fr = accS.get()
            pe.wait(hd_tok[h], zt, fr)
            near = []
            if kt * 128 >= q0:
                near.append((c0, 0))
            if q0 <= kt * 128 + 128 < q0 + QC:
                near.append((kt * 128 + 128 - q0, 128))
            for Sb, qp in ((S1, qp1[s]), (S2, qp2[s])):
                ins = pe.h.matmul(Sb[:, c0:QC], lhsT=kTs[s][:, kt * 128:(kt + 1) * 128], rhs=qp[:, q0 + c0:q0 + QC],
                                  start=True, stop=(len(near) == 0))
                for i, (cs, go) in enumerate(near):
                    ins = pe.h.matmul(Sb[:, cs:cs + 128], lhsT=ident[:, :], rhs=dr["G_sb"][:, h, go:go + 128],
                                      start=False, stop=(i == len(near) - 1))
            return bi, S1, S2, pe.mark(ins), c0

        def emit_exp(h, sinfo):
            bi, S1, S2, tmm, c0 = sinfo
            pi, (P1, P2), pfr = Prot.get()
            act.wait(tmm, pfr)
            t1 = act.mark(act.h.activation(out=P1[:, c0:QC], in_=S1[:, c0:QC], func=AF.Exp, scale=0.125,
                                           bias=dr["cb_sb"][:, h:h + 1]))
            t2 = act.mark(act.h.activation(out=P2[:, c0:QC], in_=S2[:, c0:QC], func=AF.Exp, scale=0.125,
                                           bias=dr["cb_sb"][:, h:h + 1]))
            accS.release(bi, t2)
            return pi, P1, P2, t2, c0

        def emit_PV(h, kt, nkt, pinfo):
            s = h % 2
            pi, P1, P2, texp, c0 = pinfo
            pe.wait(texp)
            if kt == 0:
                pe.wait(state["O_free"])
            first, last = (kt == 0), (kt == nkt - 1)
            for Ob, sbk, Pm in ((O1, s1, P1), (O2, s2, P2)):
                pe.h.matmul(Ob[:, c0:QC], lhsT=Vs[s][:, kt, :], rhs=Pm[:, c0:QC], start=first, stop=last)
                ins = pe.h.matmul(sbk[:, c0:QC], lhsT=ones_bf[:, :], rhs=Pm[:, c0:QC], start=first, stop=last)
            t = pe.mark(ins)
            Prot.release(pi, t)
            return t

        def emit_finalize(h, qc, tlast):
            q0 = qc * QC
            dve.wait(tlast, state["fin_free"])
            t = dve.mark(dve.h.reciprocal(out=r1[:, :], in_=s1[:, :]))
            dve.wait(t)
            t = dve.mark(dve.h.tensor_tensor(out=ta[:, :], in0=O1[:, :], in1=r1[:, :], op=ALU.mult))
            t = dve.mark(dve.h.reciprocal(out=r2[:, :], in_=s2[:, :]))
            dve.wait(t)
            t = dve.mark(dve.h.tensor_tensor(out=tb[:, :], in0=O2[:, :], in1=r2[:, :], op=ALU.mult))
            state["O_free"] = [t]
            dve.wait(t)
            to_tok = dve.mark(dve.h.scalar_tensor_tensor(out=to[:, :], in0=tb[:, :], scalar=dr["neglam_sb"][:, l:l + 1],
                                                         in1=ta[:, :], op0=ALU.mult, op1=ALU.add))
            act.wait(to_tok)
            tsq_tok = act.mark(act.h.activation(out=tsq[:, :], in_=to[:, :], func=AF.Square))

            def part2():
                bi = accS.i % len(accS.items)
                (B1, B2), fr = accS.items[bi], accS.free[bi]
                pe.wait(tsq_tok, fr)
                tm = pe.mark(pe.h.matmul(B1[:, :], lhsT=ones_f[:, :], rhs=tsq[:, :], start=True, stop=True))
                act.wait(tm)
                t = act.mark(act.h.activation(out=trs[:, :], in_=B1[:, :], func=AF.Sqrt, scale=1.0 / 128,
                                              bias=dr["eps_sb"][:, 0:1]))
                accS.release(bi, t)
                dve.wait(t)
                t = dve.mark(dve.h.reciprocal(out=trs[:, :], in_=trs[:, :]))
                yi, ys, yfr = ysr.get()
                dve.wait(t, yfr)
                t = dve.mark(dve.h.scalar_tensor_tensor(out=ys[:, :], in0=to[:, :], scalar=dr["gAs_sb"][:, l:l + 1],
                                                        in1=trs[:, :], op0=ALU.mult, op1=ALU.mult))
                state["fin_free"] = [t]
                sp.wait(t)
                td = sp.dma(mixT[h * 128:(h + 1) * 128, q0:q0 + QC], ys[:, :], ys_sems[yi])
                ysr.release(yi, td)
            deferred.append([2, part2])

        def tick_deferred(force=False):
            for d in list(deferred):
                d[0] -= 1
                if d[0] <= 0 or force:
                    d[1]()
                    deferred.remove(d)

        issue_head_loads(0)
        for h in range(4):
            if h + 1 < 4:
                issue_head_loads(h + 1)
            pairs = [(qc, kt) for qc in range(NQ) for kt in range(4 * qc + 4)]
            sinfo = emit_S(h, *pairs[0])
            for j, (qc, kt) in enumerate(pairs):
                nxt = emit_S(h, *pairs[j + 1]) if j + 1 < len(pairs) else None
                pinfo = emit_exp(h, sinfo)
                nkt = 4 * qc + 4
                tpv = emit_PV(h, kt, nkt, pinfo)
                tick_deferred()
                if kt == nkt - 1:
                    emit_finalize(h, qc, tpv)
                sinfo = nxt
            tick_deferred(force=True)
            hd_free[h % 2] = [tpv]
        k.barrier()


def phase_C(k, l, dr):
    nc = k.nc
    pe, act, dve, pool, sp = k.pe, k.act, k.dve, k.pool, k.sp
    NT = S // 128
    mixT = dr["mixT"]
    with ExitStack() as es:
        def sb(name, shape, dt):
            return es.enter_context(nc.sbuf_tensor(f"{name}_L{l}", shape, dt))
        gt = [sb(f"Cg{i}", [128, 512], F32) for i in range(2)]
        kkt = [sb(f"Ckk{i}", [128, 512], BF16) for i in range(2)]
        Vt = [sb(f"CV{i}", [128, 512], BF16) for i in range(2)]
        rqt = [sb(f"Crq{i}", [128, 4, 128], BF16) for i in range(2)]
        rgt = [sb(f"Crg{i}", [128, 4, 128], BF16) for i in range(2)]
        EA = sb("CEA", [128, 512], F32)
        EnA = sb("CEnA", [128, 512], F32)
        EH = sb("CEH", [128, 512], F32)
        ee = sb("Cee", [128, 4, 4], F32)
        qtl = sb("Cqtl", [128, 4, 128], BF16)
        ktl = sb("Cktl", [128, 4, 128], BF16)
        khb = sb("Ckhb", [128, 2, 512], BF16)
        attb = sb("Cattb", [128, 2, 4, 64], BF16)
        Smid = sb("CSmid", [128, 4, 128], BF16)
        stt = sb("Cstate", [128, 4, 128], F32)
        tmp = sb("Ctmp", [128, 4, 128], F32)
        sq = sb("Csq", [128, 512], F32)
        rsn = sb("Crsn", [128, 512], F32)
        y1 = sb("Cy1", [128, 2, 4, 64], F32)
        ystg = [sb(f"Cys{i}", [128, 4, 128], BF16) for i in range(2)]

        init = [pool.mark(pool.h.memset(khb[:, :, :], 0.0)),
                pool.mark(pool.h.memset(attb[:, :, :, :], 0.0)),
                pool.mark(pool.h.memset(stt[:, :, :], 0.0))]
        AT, EB, MB, KB, ATT, OB, UB, NB = k.ps
        KBv = KB[:, :].bitcast(BF16)

        ld_sem = [k.sem(f"C_ld{i}") for i in range(2)]
        ld_free = [[], []]
        ld_tok = {}

        def issue_loads(tt):
            s = tt % 2
            t0 = tt * 128
            sp.wait(ld_free[s])
            sp.dma(gt[s][:, :], dr["g"][t0:t0 + 128, :], ld_sem[s])
            sp.dma(kkt[s][:, :], dr["kk"][t0:t0 + 128, :], ld_sem[s])
            sp.dma(Vt[s][:, :], dr["ri"][t0:t0 + 128, :], ld_sem[s])
            sp.dma(rqt[s][:, :, :], dr["rqT"][:, :, t0:t0 + 128].rearrange("h p t -> p h t"), ld_sem[s])
            ld_tok[tt] = sp.dma(rgt[s][:, :, :], dr["rgT"][:, :, t0:t0 + 128].rearrange("h p t -> p h t"), ld_sem[s])

        ys_sems = [k.sem(f"C_ys{i}") for i in range(2)]
        ysr = Rot(ystg)
        rd = {}

        def R(name):
            return rd.get(name, [])

        M1, Em, M2, tri = dr["M1_sb"], dr["Em_sb"], dr["M2_sb"], dr["tri_sb"]
        tok_state = init[2]
        issue_loads(0)
        for tt in range(NT):
            s = tt % 2
            t0 = tt * 128
            if tt + 1 < NT:
                issue_loads(tt + 1)
            tl = ld_tok[tt]
            pe.wait(tl, R("AT"))
            for h in range(4):
                ins = pe.h.matmul(AT[:, h * 128:(h + 1) * 128], lhsT=gt[s][:, h * 128:(h + 1) * 128], rhs=M1[:, :],
                                  start=True, stop=True)
            t_AT = pe.mark(ins)
            pe.wait(R("EB"))
            for h in range(4):
                ins = pe.h.matmul(EB[:, h * 4:(h + 1) * 4], lhsT=gt[s][:, h * 128:(h + 1) * 128], rhs=Em[:, :],
                                  start=True, stop=True)
            t_EB = pe.mark(ins)
            pe.wait(R("MB"))
            t_MB = pe.mark(pe.h.matmul(MB[:, :], lhsT=M2[:, :], rhs=gt[s][:, :], start=True, stop=True))
            pe.wait(R("KB"))
            for h in range(4):
                ins = pe.h.transpose(out=KBv[:, h * 128:(h + 1) * 128], in_=kkt[s][:, h * 128:(h + 1) * 128],
                                     identity=dr["ident"][:, :])
            t_KB = pe.mark(ins)
            act.wait(t_AT, R("EA"))
            t_EA = act.mark(act.h.activation(out=EA[:, :], in_=AT[:, :], func=AF.Exp))
            act.wait(R("EnA"))
            t_EnA = act.mark(act.h.activation(out=EnA[:, :], in_=AT[:, :], func=AF.Exp, scale=-1.0))
            rd["AT"] = [t_EnA]
            act.wait(t_MB, R("EH"))
            t_EH = act.mark(act.h.activation(out=EH[:, :], in_=MB[:, :], func=AF.Exp))
            rd["MB"] = [t_EH]
            act.wait(t_EB, R("ee"))
            t_ee = act.mark(act.h.activation(out=ee[:, :, :], in_=EB[:, 0:16].rearrange("p (h j) -> p h j", h=4),
                                             func=AF.Exp))
            rd["EB"] = [t_ee]
            pool.wait(t_EA, tl, R("qtl"))
            t_qtl = pool.mark(pool.h.tensor_tensor(out=qtl[:, :, :], in0=rqt[s][:, :, :],
                                                   in1=EA[:, :].rearrange("p (h t) -> p h t", h=4), op=ALU.mult))
            rd["EA"] = [t_qtl]
            dve.wait(t_KB, t_EnA, R("ktl"))
            t_ktl = dve.mark(dve.h.tensor_tensor(out=ktl[:, :, :], in0=KBv[:, 0:512].rearrange("p (h t) -> p h t", h=4),
                                                 in1=EnA[:, :].rearrange("p (h t) -> p h t", h=4), op=ALU.mult))
            rd["KB"] = [t_ktl]
            rd["EnA"] = [t_ktl]
            pool.wait(t_EH, R("khb"), init[0])
            for c in range(2):
                t_khb = pool.mark(pool.h.tensor_tensor(out=khb[c * 64:(c + 1) * 64, c, :], in0=kkt[s][c * 64:(c + 1) * 64, :],
                                                       in1=EH[c * 64:(c + 1) * 64, :], op=ALU.mult))
            rd["EH"] = [t_khb]
            pe.wait(t_qtl, t_ktl, R("ATT"))
            for c in range(2):
                for h in range(4):
                    ins = pe.h.matmul(ATT[:, c * 256 + h * 64:c * 256 + (h + 1) * 64], lhsT=ktl[:, h, :],
                                      rhs=qtl[:, h, c * 64:(c + 1) * 64], start=True, stop=True)
            t_ATT = pe.mark(ins)
            rd["ktl"] = [t_ATT]
            dve.wait(t_ATT, R("attb"), init[1])
            for c in range(2):
                t_attb = dve.mark(dve.h.tensor_tensor(
                    out=attb[c * 64:(c + 1) * 64, c, :, :],
                    in0=ATT[c * 64:(c + 1) * 64, c * 256:(c + 1) * 256].rearrange("p (h t) -> p h t", h=4),
                    in1=tri[c * 64:(c + 1) * 64, :, :], op=ALU.mult))
            rd["ATT"] = [t_attb]
            eev = ee[:, :, :]
            for c in range(2):
                dve.wait(tok_state, t_ee, R("Smid"))
                t_Smid = dve.mark(dve.h.tensor_tensor(out=Smid[:, :, :], in0=stt[:, :, :],
                                                      in1=bcast_mid(eev[:, :, c], 128), op=ALU.mult))
                pe.wait(t_attb, t_Smid, tl, R("OB") if c == 0 else [])
                for h in range(4):
                    o_ap = OB[:, c * 256 + h * 64:c * 256 + (h + 1) * 64]
                    pe.h.matmul(o_ap, lhsT=Vt[s][:, h * 128:(h + 1) * 128], rhs=attb[:, c, h, :], start=True, stop=False)
                    ins = pe.h.matmul(o_ap, lhsT=Smid[:, h, :], rhs=qtl[:, h, c * 64:(c + 1) * 64], start=False, stop=True)
                t_OB = pe.mark(ins)
                rd["Smid"] = [t_OB]
                pe.wait(t_khb, R("UB"))
                for h in range(4):
                    ins = pe.h.matmul(UB[:, h * 128:(h + 1) * 128], lhsT=khb[:, c, h * 128:(h + 1) * 128],
                                      rhs=Vt[s][:, h * 128:(h + 1) * 128], start=True, stop=True)
                t_UB = pe.mark(ins)
                pool.wait(tok_state, t_ee, t_Smid, R("tmp"))
                t_tmp = pool.mark(pool.h.tensor_tensor(out=tmp[:, :, :], in0=stt[:, :, :],
                                                       in1=bcast_mid(eev[:, :, 2 + c], 128), op=ALU.mult))
                dve.wait(t_tmp, t_UB, t_Smid)
                tok_state = dve.mark(dve.h.tensor_tensor(out=stt[:, :, :], in0=tmp[:, :, :],
                                                         in1=UB[:, :].rearrange("p (h v) -> p h v", h=4), op=ALU.add))
                rd["UB"] = [tok_state]
                rd["tmp"] = [tok_state]
            rd["attb"] = [t_OB]
            rd["qtl"] = [t_OB]
            rd["khb"] = [t_UB]
            rd["ee"] = [tok_state]
            act.wait(t_OB, R("sq"))
            t_sq = act.mark(act.h.activation(out=sq[:, :], in_=OB[:, :], func=AF.Square))
            pe.wait(t_sq, R("NB"))
            t_NB = pe.mark(pe.h.matmul(NB[:, :], lhsT=dr["ones_f"][:, :], rhs=sq[:, :], start=True, stop=True))
            rd["sq"] = [t_NB]
            act.wait(t_NB, R("rsn"))
            t = act.mark(act.h.activation(out=rsn[:, :], in_=NB[:, :], func=AF.Sqrt, scale=1.0 / 128,
                                          bias=dr["eps_sb"][:, 0:1]))
            rd["NB"] = [t]
            dve.wait(t)
            t = dve.mark(dve.h.reciprocal(out=rsn[:, :], in_=rsn[:, :]))
            dve.wait(t, R("y1"))
            t_y1 = dve.mark(dve.h.scalar_tensor_tensor(out=y1[:, :, :, :],
                                                       in0=OB[:, :].rearrange("p (c h t) -> p c h t", c=2, h=4),
                                                       scalar=dr["gH_sb"][:, l:l + 1],
                                                       in1=rsn[:, :].rearrange("p (c h t) -> p c h t", c=2, h=4),
                                                       op0=ALU.mult, op1=ALU.mult))
            rd["OB"] = [t_y1]
            rd["rsn"] = [t_y1]
            yi, ys, yfr = ysr.get()
            pool.wait(t_y1, yfr)
            t_ys = pool.mark(pool.h.tensor_tensor(out=ys[:, :, :].rearrange("p h (c t) -> p c h t", c=2),
                                                  in0=y1[:, :, :, :],
                                                  in1=rgt[s][:, :, :].rearrange("p h (c t) -> p c h t", c=2), op=ALU.mult))
            rd["y1"] = [t_ys]
            sp.wait(t_ys)
            td = sp.dma(mixT[512:1024, t0:t0 + 128].rearrange("(h p) t -> p h t", p=128), ys[:, :, :], ys_sems[yi])
            ysr.release(yi, td)
            ld_free[s] = [t_ys, t_UB, t_OB, t_KB, t_MB]
        k.barrier()


def t5_bucket_np(rel):
    n_half, max_exact = 16, 8
    ret = np.where(rel > 0, n_half, 0)
    n = np.abs(rel)
    nf = np.maximum(n, 1).astype(np.float32)
    large = max_exact + (np.log(nf / np.float32(max_exact)) / np.float32(math.log(128 / max_exact))
                         * np.float32(n_half - max_exact)).astype(np.int32)
    large = np.minimum(large, n_half - 1)
    return ret + np.where(n < max_exact, n, large)


LAM_INIT = [0.8 - 0.6 * math.exp(-0.3 * l) for l in range(DEPTH)]

CF = {}
_o = 0
for _n, _w in (("ones_f", 128), ("M1", 128), ("M2", 128), ("Em", 4), ("eps", 1), ("tri", 256), ("OH8", 384),
               ("maskadd", 128), ("oml_init", 4), ("neg_lam_init", 4)):
    CF[_n] = (_o, _w)
    _o += _w
CF_W = _o
CB = {"ident": (0, 128), "J": (128, 128), "ones_bf": (256, 128)}
CB_W = 384


def host_consts():
    cf = np.zeros((128, CF_W), np.float32)

    def put(name, arr):
        o, w = CF[name]
        cf[:arr.shape[0], o:o + w] = arr.reshape(arr.shape[0], -1)

    put("ones_f", np.ones((128, 128), np.float32))
    tp = np.arange(128)[:, None]
    t = np.arange(128)[None, :]
    same = (tp // 64) == (t // 64)
    mid = (t // 64) * 64 + 31
    put("M1", (same & (tp <= t)).astype(np.float32) - (same & (tp <= mid)).astype(np.float32))
    put("M2", (same & (tp > t)).astype(np.float32))
    Em = np.zeros((128, 4), np.float32)
    ar = np.arange(128)
    for c in range(2):
        Em[:, c] = ((ar // 64 == c) & (ar <= c * 64 + 31)).astype(np.float32)
        Em[:, 2 + c] = (ar // 64 == c).astype(np.float32)
    put("Em", Em)
    put("eps", np.full((128, 1), EPS, np.float32))
    s_ = (ar % 64)[:, None, None]
    tq = np.arange(64)[None, None, :]
    put("tri", np.broadcast_to((s_ <= tq), (128, 4, 64)).astype(np.float32))
    i = np.arange(384)
    rel = 127 - i
    bk = t5_bucket_np(rel)
    oh = np.zeros((32, 384), np.float32)
    oh[bk, i] += 8.0
    oh[15, :] -= 8.0
    oh[:, 383] = 0.0
    put("OH8", oh)
    p = np.arange(128)[:, None]
    j = np.arange(128)[None, :]
    put("maskadd", np.where((p // 64) > (j // 64), -240000.0, 0.0).astype(np.float32))
    put("oml_init", np.broadcast_to(np.array([1.0 - x for x in LAM_INIT], np.float32)[None, :], (128, 4)))
    put("neg_lam_init", np.broadcast_to(np.array([-x for x in LAM_INIT], np.float32)[None, :], (128, 4)))
    cb = np.zeros((128, CB_W), np.float32)
    cb[:, 0:128] = np.eye(128)
    cb[:, 128:256] = np.eye(128)[::-1]
    cb[:, 256:384] = 1.0
    return cf, cb.astype(ml_dtypes.bfloat16)


def setup(k, dr, ext):
    nc = k.nc
    pe, act, dve, pool, sp = k.pe, k.act, k.dve, k.pool, k.sp
    cs = k.sem("consts")
    cf = nc.alloc_sbuf_tensor("cf_sb", [128, CF_W], F32)
    cbt = nc.alloc_sbuf_tensor("cb_sb16", [128, CB_W], BF16)
    toks = [sp.dma(cf[:, :], ext["cf"], cs), sp.dma(cbt[:, :], ext["cb16"], cs)]

    def fv(name):
        o, w = CF[name]
        return cf[:, o:o + w]
    dr["ones_f"] = fv("ones_f")
    dr["M1_sb"] = fv("M1")
    dr["M2_sb"] = fv("M2")
    dr["Em_sb"] = fv("Em")
    dr["eps_sb"] = fv("eps")
    dr["tri_sb"] = fv("tri").rearrange("p (h t) -> p h t", h=4)
    dr["ident"] = cbt[:, 0:128]
    Jm = cbt[:, 128:256]
    dr["ones_bf"] = cbt[:, 256:384]
    for name, w in (("g1", 32), ("g2", 32), ("gA", 4), ("gH", 4)):
        t = nc.alloc_sbuf_tensor(name + "_sb", [128, w], F32)
        toks.append(sp.dma(t[:, :], ext[name], cs))
        dr[name + "_sb"] = t
    dr["final_g"] = ext["final_g"]
    neglam = nc.alloc_sbuf_tensor("neglam_sb", [128, 4], F32)
    gAs = nc.alloc_sbuf_tensor("gAs_sb", [128, 4], F32)
    cbias = nc.alloc_sbuf_tensor("cbias_sb", [128, 4], F32)
    G = nc.alloc_sbuf_tensor("G_sb", [128, 4, 256], BF16)
    dr["neglam_sb"], dr["gAs_sb"], dr["cb_sb"], dr["G_sb"] = neglam, gAs, cbias, G
    toks.append(sp.dma(cbias[:, :], ext["rel_bias"][15:16, :].partition_broadcast(128), cs))
    with ExitStack() as es:
        def sb(name, shape, dt):
            return es.enter_context(nc.sbuf_tensor(name, shape, dt))
        lg = sb("S_lg", [128, 4, 512], F32)
        lb = sb("S_lb", [128, 4, 512], F32)
        oml = sb("S_oml", [128, 4, 512], F32)
        ssum = sb("S_sum", [128, 512], F32)
        lq = sb("S_lq", [128, 4, 4, 64], F32)
        pr = sb("S_pr", [128, 4, 2, 64], F32)
        dd = sb("S_dd", [128, 4, 2], F32)
        rb = sb("S_rb", [32, 4], F32)
        tv = sb("S_tv", [4, 384], F32)
        tvb = sb("S_tvb", [4, 384], BF16)
        Hk = sb("S_H", [128, 4, 256], BF16)
        toks.append(sp.dma(lg[:, :, :], ext["lb_logits"].partition_broadcast(128), cs))
        toks.append(sp.dma(lq[:, :, :, :], ext["lam_qk"].partition_broadcast(128), cs))
        toks.append(sp.dma(rb[:, :], ext["rel_bias"], cs))
        for e in k.engs:
            e.wait(toks[-1])
        t = act.mark(act.h.activation(out=lg[:, :, :], in_=lg[:, :, :], func=AF.Exp))
        dve.wait(t)
        t = dve.mark(dve.h.tensor_tensor(out=ssum[:, :], in0=lg[:, 0, :], in1=lg[:, 1, :], op=ALU.add))
        dve.wait(t)
        t = dve.mark(dve.h.tensor_tensor(out=ssum[:, :], in0=ssum[:, :], in1=lg[:, 2, :], op=ALU.add))
        dve.wait(t)
        t = dve.mark(dve.h.tensor_tensor(out=ssum[:, :], in0=ssum[:, :], in1=lg[:, 3, :], op=ALU.add))
        dve.wait(t)
        t = dve.mark(dve.h.reciprocal(out=ssum[:, :], in_=ssum[:, :]))
        dve.wait(t)
        t = dve.mark(dve.h.memset(lb[:, 0, :], 0.0))
        t = dve.mark(dve.h.tensor_tensor(out=lb[:, 1, :], in0=lg[:, 1, :], in1=ssum[:, :], op=ALU.mult))
        t = dve.mark(dve.h.tensor_tensor(out=lb[:, 2, :], in0=lg[:, 2, :], in1=ssum[:, :], op=ALU.mult))
        t = dve.mark(dve.h.tensor_tensor(out=lb[:, 3, :], in0=lg[:, 3, :], in1=ssum[:, :], op=ALU.mult))
        dve.wait(t)
        t = dve.mark(dve.h.tensor_tensor(out=lb[:, 2, :], in0=lb[:, 2, :], in1=lb[:, 1, :], op=ALU.add))
        dve.wait(t)
        t = dve.mark(dve.h.tensor_tensor(out=lb[:, 3, :], in0=lb[:, 3, :], in1=lb[:, 2, :], op=ALU.add))
        dve.wait(t)
        t = dve.mark(dve.h.tensor_scalar(out=oml[:, :, :], in0=lb[:, :, :], scalar1=-1.0, scalar2=1.0,
                                         op0=ALU.mult, op1=ALU.add))
        sp.wait(t)
        sp.dma(dr["lbrep"], lb[:, :, :], cs)
        sp.dma(dr["omlrep"], oml[:, :, :], cs)
        t = dve.mark(dve.h.tensor_tensor(out=pr[:, :, 0, :], in0=lq[:, :, 0, :], in1=lq[:, :, 1, :], op=ALU.mult))
        t = dve.mark(dve.h.tensor_tensor(out=pr[:, :, 1, :], in0=lq[:, :, 2, :], in1=lq[:, :, 3, :], op=ALU.mult))
        dve.wait(t)
        t = dve.mark(dve.h.reduce_sum(out=dd[:, :, :], in_=pr[:, :, :, :], axis=AX.X))
        act.wait(t)
        t = act.mark(act.h.activation(out=dd[:, :, :], in_=dd[:, :, :], func=AF.Exp))
        dve.wait(t)
        t = dve.mark(dve.h.tensor_tensor(out=neglam[:, :], in0=dd[:, :, 1], in1=dd[:, :, 0], op=ALU.subtract))
        dve.wait(t)
        t = dve.mark(dve.h.tensor_tensor(out=neglam[:, :], in0=neglam[:, :], in1=fv("neg_lam_init"), op=ALU.add))
        t = dve.mark(dve.h.tensor_tensor(out=gAs[:, :], in0=dr["gA_sb"][:, :], in1=fv("oml_init"), op=ALU.mult))
        o8, w8 = CF["OH8"]
        tm = pe.mark(pe.h.matmul(k.ps[0][0:4, 0:384], lhsT=rb[:, :], rhs=cf[0:32, o8:o8 + w8], start=True, stop=True))
        dve.wait(tm)
        t = dve.mark(dve.h.tensor_copy(out=tvb[:, :], in_=k.ps[0][0:4, 0:384]))
        sp.wait(t)
        scr = dr["tvec"]
        t = sp.dma(scr, tvb[:, :], cs)
        sp.wait(t)
        hank = bass.AP(tensor=scr.tensor, offset=scr.offset, ap=[[1, 128], [384, 4], [1, 256]])
        t = sp.dma(Hk[:, :, :], hank, cs)
        pe.wait(t)
        pe.h.matmul(k.ps[1][:, :], lhsT=Jm, rhs=Hk[:, 0:2, :], start=True, stop=True)
        tm = pe.mark(pe.h.matmul(k.ps[2][:, :], lhsT=Jm, rhs=Hk[:, 2:4, :], start=True, stop=True))
        dve.wait(tm)
        for hh, bank in ((0, k.ps[1]), (1, k.ps[2])):
            bv = bank[:, :].rearrange("p (h j) -> p h j", h=2)
            t = dve.mark(dve.h.tensor_tensor(out=G[:, 2 * hh:2 * hh + 2, 0:128], in0=bv[:, :, 0:128],
                                             in1=fv("maskadd").unsqueeze(1).broadcast_to([128, 2, 128]), op=ALU.add))
            t = dve.mark(dve.h.tensor_copy(out=G[:, 2 * hh:2 * hh + 2, 128:256], in_=bv[:, :, 128:256]))
        k.barrier()


def build_full(nlayers=DEPTH, debug=False):
    nc = bass.Bass("TRN2", target_bir_lowering=False)
    k = K(nc)

    def din(name, shape, dt=F32):
        return nc.dram_tensor(name, shape, dt, kind="ExternalInput").ap()

    def dscr(name, shape, dt):
        return nc.dram_tensor(name, shape, dt, kind="Internal").ap()

    ext = {
        "x": din("x", [S, D]),
        "w_in": din("w_in", [DEPTH, D, INC]),
        "w_out": din("w_out", [DEPTH, D, D]),
        "w_up": din("w_up", [DEPTH, D, DFF]),
        "w_down": din("w_down", [DEPTH, DFF, D]),
        "lb_logits": din("lb_logits", [DEPTH, 512]),
        "lam_qk": din("lam_qk", [DEPTH, 4, 64]),
        "rel_bias": din("rel_bias", [32, 4]),
        "final_g": din("final_g", [D]),
        "g1": din("g1", [128, 32]),
        "g2": din("g2", [128, 32]),
        "gA": din("gA", [128, 4]),
        "gH": din("gH", [128, 4]),
        "cf": din("cf", [128, CF_W]),
        "cb16": din("cb16", [128, CB_W], BF16),
    }
    y = nc.dram_tensor("y", [S, D], F32, kind="ExternalOutput").ap()
    dr = {"y": y}
    for n in ("w_in", "w_out", "w_up", "w_down"):
        dr[n] = ext[n]
    xres = dscr("xres", [S, D], F32)
    for n in ("qT", "kT", "rqT", "rgT"):
        dr[n] = dscr(n, [4, 128, S], BF16)
    for n in ("v", "ri", "kk"):
        dr[n] = dscr(n, [S, 512], BF16)
    dr["g"] = dscr("g", [S, 512], F32)
    dr["mixT"] = dscr("mixT", [1024, S], BF16)
    dr["tvec"] = dscr("tvec", [4, 384], BF16)
    dr["lbrep"] = dscr("lbrep", [128, DEPTH, 512], F32)
    dr["omlrep"] = dscr("omlrep", [128, DEPTH, 512], F32)
    setup(k, dr, ext)
    for l in range(nlayers):
        dr["x_in"] = ext["x"] if l == 0 else xres
        dr["x_out"] = xres
        phase_A(k, l, dr)
        phase_B(k, l, dr)
        phase_C(k, l, dr)
        phase_D(k, l, dr, final=(l == nlayers - 1))
    k.barrier()
    return nc


def make_in_maps(inputs):
    cf, cb16 = host_consts()
    f32 = lambda a: np.ascontiguousarray(np.asarray(a, dtype=np.float32))
    shared = {
        "w_in": f32(inputs["w_in"]), "w_out": f32(inputs["w_out"]),
        "w_up": f32(inputs["w_up"]), "w_down": f32(inputs["w_down"]),
        "lb_logits": f32(inputs["lb_logits"]), "lam_qk": f32(inputs["lam_qk"]),
        "rel_bias": f32(inputs["rel_bias"]), "final_g": f32(inputs["final_g"]),
        "g1": f32(np.asarray(inputs["norm1_g"]).reshape(DEPTH, 8, 128).transpose(2, 0, 1).reshape(128, 32)),
        "g2": f32(np.asarray(inputs["norm2_g"]).reshape(DEPTH, 8, 128).transpose(2, 0, 1).reshape(128, 32)),
        "gA": f32(np.asarray(inputs["attn_norm_g"]).T),
        "gH": f32(np.asarray(inputs["hgrn_norm_g"]).T),
        "cf": cf, "cb16": cb16,
    }
    x = f32(inputs["x"])
    return [dict(shared, x=x[b]) for b in range(x.shape[0])]


_NC_CACHE = {}


def kernel(**inputs):
    if "nc" not in _NC_CACHE:
        _NC_CACHE["nc"] = build_full()
    nc = _NC_CACHE["nc"]
    in_maps = make_in_maps(inputs)
    res = run_bass_kernel_spmd(nc, in_maps, core_ids=list(range(NCORES)))
    return np.stack([np.asarray(r["y"], dtype=np.float32) for r in res.results], axis=0)
```

```python
import math
from contextlib import ExitStack

import numpy as np
import ml_dtypes
import concourse.bass as bass
import concourse.mybir as mybir
from concourse.bass_utils import run_bass_kernel_spmd

F32 = mybir.dt.float32
BF16 = mybir.dt.bfloat16
AF = mybir.ActivationFunctionType
ALU = mybir.AluOpType
AX = mybir.AxisListType

S = 4096
D = 1024
DEPTH = 4
DFF = 4096
INC = 3584
EPS = 1e-6
NCORES = 8


class Sem:
    def __init__(self, nc, name):
        self.h = nc.alloc_semaphore(name)
        self.cnt = 0
        self.name = name


class Eng:
    def __init__(self, k, h, name):
        self.k = k
        self.h = h
        self.sem = Sem(k.nc, "e_" + name)
        self.seen = {}

    def wait(self, *toks):
        for t in toks:
            if t is None:
                continue
            if isinstance(t, list):
                self.wait(*t)
                continue
            sem, val = t
            if self.seen.get(sem.name, 0) >= val:
                continue
            self.h.wait_ge(sem.h, val)
            self.seen[sem.name] = val

    def mark(self, inst):
        self.sem.cnt += 1
        inst.then_inc(self.sem.h, 1)
        tok = (self.sem, self.sem.cnt)
        self.k.latest[self.sem.name] = tok
        return tok

    def dma(self, out, in_, sem):
        inst = self.h.dma_start(out=out, in_=in_)
        sem.cnt += 16
        inst.then_inc(sem.h, 16)
        tok = (sem, sem.cnt)
        self.k.latest[sem.name] = tok
        return tok


class Rot:
    def __init__(self, items):
        self.items = items
        self.i = 0
        self.free = [[] for _ in items]

    def get(self):
        idx = self.i % len(self.items)
        self.i += 1
        return idx, self.items[idx], self.free[idx]

    def release(self, idx, *toks):
        self.free[idx] = list(toks)


class K:
    def __init__(self, nc):
        self.nc = nc
        self.latest = {}
        self.pe = Eng(self, nc.tensor, "pe")
        self.act = Eng(self, nc.scalar, "act")
        self.dve = Eng(self, nc.vector, "dve")
        self.pool = Eng(self, nc.gpsimd, "pool")
        self.sp = Eng(self, nc.sync, "sp")
        self.engs = [self.pe, self.act, self.dve, self.pool, self.sp]
        self.sems = {}
        self.psum = nc.alloc_psum_tensor("psall", [128, 4096], F32)
        self.ps = [self.psum[:, i * 512:(i + 1) * 512] for i in range(8)]

    def sem(self, name):
        if name not in self.sems:
            self.sems[name] = Sem(self.nc, name)
        return self.sems[name]

    def barrier(self):
        toks = list(self.latest.values())
        for e in self.engs:
            e.wait(*toks)


def bcast_mid(ap2d, n):
    p, a = ap2d.shape
    return ap2d.unsqueeze(2).broadcast_to([p, a, n])


def phase_D(k, l, dr, final):
    nc = k.nc
    pe, act, dve, pool, sp = k.pe, k.act, k.dve, k.pool, k.sp
    TB = 256
    NB = S // TB
    xin = dr["x_in"]
    xout = dr["x_out"]
    mixT = dr["mixT"]
    with ExitStack() as es:
        def sb(name, shape, dt):
            return es.enter_context(nc.sbuf_tensor(f"{name}_L{l}", shape, dt))
        w_out_sb = sb("w_out_sb", [128, 8, 1024], BF16)
        w_up_sb = sb("w_up_sb", [128, 8, 4096], BF16)
        w_dn_sb = sb("w_dn_sb", [128, 32, 1024], BF16)
        mT = [sb("mT0", [128, 8, TB], BF16)]
        xs = [sb(f"xs{i}", [128, 2, 1024], F32) for i in range(2)]
        h2 = sb("h2", [128, 2, 1024], BF16)
        h2T = sb("h2T", [128, 8, TB], BF16)
        u2T = sb("u2T", [128, 32, TB], BF16)
        sq = [sb(f"sq{i}", [128, TB], F32) for i in range(2)]
        ss = sb("ssD", [128, 4], F32)
        rs = sb("rsD", [128, 4], F32)
        rstd = sb("rstdD", [128, 4], F32)
        if final:
            gF = sb("gF_sb", [128, 1024], F32)
            tok_gF = sp.dma(gF[:, :], dr["final_g"].partition_broadcast(128), k.sem("consts"))

        wtok_out = pool.dma(w_out_sb[:, :, :], dr["w_out"][l].rearrange("(c p) d -> p c d", p=128), k.sem("w_out"))
        wtok_up = []
        for dc in range(8):
            wtok_up.append(pool.dma(w_up_sb[:, dc, :], dr["w_up"][l, dc * 128:(dc + 1) * 128, :], k.sem(f"w_up{dc % 4}")))
        wtok_dn = []
        for j in range(4):
            wtok_dn.append(pool.dma(w_dn_sb[:, j * 8:(j + 1) * 8, :],
                                    dr["w_down"][l, j * 1024:(j + 1) * 1024, :].rearrange("(c p) d -> p c d", p=128),
                                    k.sem(f"w_dn{j}")))

        mT_sem = [k.sem("D_mT0")]
        xs_sem = [k.sem(f"D_xs{i}") for i in range(2)]
        st_sem = [k.sem(f"D_st{i}") for i in range(2)]
        mT_free = [[]]
        xs_free = [[], []]
        ld_tok = {}
        ldm_tok = {}

        def issue_mT(b):
            t0 = b * TB
            sp.wait(mT_free[0])
            ldm_tok[b] = sp.dma(mT[0][:, :, :], mixT.rearrange("(c p) t -> p c t", p=128)[:, :, t0:t0 + TB], mT_sem[0])

        def issue_loads(b):
            s = b % 2
            t0 = b * TB
            sp.wait(xs_free[s])
            ld_tok[b] = sp.dma(xs[s][:, :, :], xin[t0:t0 + TB, :].rearrange("(t p) d -> p t d", p=128), xs_sem[s])

        accA = Rot([k.ps[0], k.ps[1]])
        accU = Rot([k.ps[2], k.ps[3]])
        accT = Rot([k.ps[4], k.ps[5]])
        sqr = Rot(sq)
        h2_free = []
        h2T_free = []
        u2T_free = []
        g2 = dr["g2_sb"][:, l * 8:(l + 1) * 8]

        issue_loads(0)
        issue_mT(0)
        for b in range(NB):
            s = b % 2
            t0 = b * TB
            if b + 1 < NB:
                issue_loads(b + 1)
            tok_mT, tok_xs = ldm_tok[b], ld_tok[b]
            x1 = [[None, None], [None, None]]
            mm_reads = []
            for tt in range(2):
                for hf in range(2):
                    bi, bank, fr = accA.get()
                    pe.wait(tok_mT, wtok_out, fr)
                    for c in range(8):
                        ins = pe.h.matmul(bank[:, :], lhsT=mT[0][:, c, tt * 128:(tt + 1) * 128],
                                          rhs=w_out_sb[:, c, hf * 512:(hf + 1) * 512], start=(c == 0), stop=(c == 7))
                    tmm = pe.mark(ins)
                    mm_reads.append(tmm)
                    dve.wait(tmm, tok_xs)
                    t = dve.mark(dve.h.tensor_tensor(out=xs[s][:, tt, hf * 512:(hf + 1) * 512], in0=bank[:, :],
                                                     in1=xs[s][:, tt, hf * 512:(hf + 1) * 512], op=ALU.add))
                    accA.release(bi, t)
                    x1[tt][hf] = t
            mT_free[0] = mm_reads
            if b + 1 < NB:
                issue_mT(b + 1)
            tok_h2 = []
            for tt in range(2):
                act.wait(x1[tt][0], x1[tt][1], h2_free)
                t = act.mark(act.h.activation(out=h2[:, tt, :], in_=xs[s][:, tt, :], func=AF.Square,
                                              accum_out=ss[:, tt:tt + 1]))
                act.wait(t)
                t = act.mark(act.h.activation(out=rs[:, tt:tt + 1], in_=ss[:, tt:tt + 1], func=AF.Sqrt,
                                              scale=1.0 / D, bias=dr["eps_sb"][:, 0:1]))
                dve.wait(t)
                t = dve.mark(dve.h.reciprocal(out=rstd[:, tt:tt + 1], in_=rs[:, tt:tt + 1]))
                act.wait(t, h2_free)
                t = act.mark(act.h.activation(out=h2[:, tt, :], in_=xs[s][:, tt, :], func=AF.Copy,
                                              scale=rstd[:, tt:tt + 1]))
                tok_h2.append(t)
            tok_h2T = []
            h2_reads = []
            for tt in range(2):
                bi, bank, fr = accT.get()
                bv = bank[:, :].bitcast(BF16)
                pe.wait(tok_h2[tt], fr)
                for c in range(8):
                    ins = pe.h.transpose(out=bv[:, c * 128:(c + 1) * 128], in_=h2[:, tt, c * 128:(c + 1) * 128],
                                         identity=dr["ident"][:, :])
                tmm = pe.mark(ins)
                h2_reads.append(tmm)
                dve.wait(tmm, h2T_free)
                t = dve.mark(dve.h.tensor_tensor(out=h2T[:, :, tt * 128:(tt + 1) * 128],
                                                 in0=bv.rearrange("p (c t) -> p c t", c=8),
                                                 in1=bcast_mid(g2, 128), op=ALU.mult))
                accT.release(bi, t)
                tok_h2T.append(t)
            h2_free = h2_reads
            tok_u = []
            up_reads = []
            for fc in range(32):
                bi, bank, fr = accU.get()
                pe.wait(tok_h2T, wtok_up, fr)
                for c in range(8):
                    ins = pe.h.matmul(bank[:, 0:TB], lhsT=w_up_sb[:, c, fc * 128:(fc + 1) * 128],
                                      rhs=h2T[:, c, :], start=(c == 0), stop=(c == 7))
                tmm = pe.mark(ins)
                up_reads.append(tmm)
                si, sqt, sfr = sqr.get()
                act.wait(tmm, sfr)
                ta = act.mark(act.h.activation(out=sqt[:, :], in_=bank[:, 0:TB], func=AF.Square))
                dve.wait(ta, u2T_free)
                t = dve.mark(dve.h.scalar_tensor_tensor(out=u2T[:, fc, :], in0=bank[:, 0:TB], scalar=0.0,
                                                        in1=sqt[:, :], op0=ALU.is_gt, op1=ALU.mult))
                accU.release(bi, t)
                sqr.release(si, t)
                tok_u.append(t)
            h2T_free = [up_reads[-1]]
            x2 = []
            dn_reads = []
            for tt in range(2):
                for hf in range(2):
                    bi, bank, fr = accA.get()
                    pe.wait(tok_u[-1], wtok_dn, fr)
                    for c in range(32):
                        ins = pe.h.matmul(bank[:, :], lhsT=u2T[:, c, tt * 128:(tt + 1) * 128],
                                          rhs=w_dn_sb[:, c, hf * 512:(hf + 1) * 512], start=(c == 0), stop=(c == 31))
                    tmm = pe.mark(ins)
                    dn_reads.append(tmm)
                    dve.wait(tmm)
                    t = dve.mark(dve.h.tensor_tensor(out=xs[s][:, tt, hf * 512:(hf + 1) * 512], in0=bank[:, :],
                                                     in1=xs[s][:, tt, hf * 512:(hf + 1) * 512], op=ALU.add))
                    accA.release(bi, t)
                    x2.append(t)
            u2T_free = [dn_reads[-1]]
            if not final:
                sp.wait(x2)
                t = sp.dma(xout[t0:t0 + TB, :].rearrange("(t p) d -> p t d", p=128), xs[s][:, :, :], st_sem[s])
                xs_free[s] = [t]
            else:
                fin = []
                for tt in range(2):
                    act.wait(x2, h2_free)
                    t = act.mark(act.h.activation(out=h2[:, tt, :], in_=xs[s][:, tt, :], func=AF.Square,
                                                  accum_out=ss[:, 2 + tt:3 + tt]))
                    act.wait(t)
                    t = act.mark(act.h.activation(out=rs[:, 2 + tt:3 + tt], in_=ss[:, 2 + tt:3 + tt], func=AF.Sqrt,
                                                  scale=1.0 / D, bias=dr["eps_sb"][:, 0:1]))
                    dve.wait(t)
                    t = dve.mark(dve.h.reciprocal(out=rstd[:, 2 + tt:3 + tt], in_=rs[:, 2 + tt:3 + tt]))
                    dve.wait(t, tok_gF)
                    t = dve.mark(dve.h.scalar_tensor_tensor(out=xs[s][:, tt, :], in0=xs[s][:, tt, :],
                                                            scalar=rstd[:, 2 + tt:3 + tt], in1=gF[:, :],
                                                            op0=ALU.mult, op1=ALU.mult))
                    fin.append(t)
                sp.wait(fin)
                t = sp.dma(dr["y"][t0:t0 + TB, :].rearrange("(t p) d -> p t d", p=128), xs[s][:, :, :], st_sem[s])
                xs_free[s] = [t]
        k.barrier()


def phase_A(k, l, dr):
    nc = k.nc
    pe, act, dve, pool, sp = k.pe, k.act, k.dve, k.pool, k.sp
    TB = 512
    NB = S // TB
    xin = dr["x_in"]
    with ExitStack() as es:
        def sb(name, shape, dt):
            return es.enter_context(nc.sbuf_tensor(f"{name}_L{l}", shape, dt))
        w_in_sb = sb("w_in_sb", [128, 8, INC], BF16)
        xs = [sb(f"xa{i}", [128, 4, 1024], F32) for i in range(2)]
        junk = sb("junkA", [128, 1024], BF16)
        hs = sb("hsA", [128, 4, 1024], BF16)
        hT = sb("hTA", [128, 8, TB], BF16)
        ss = sb("ssA", [128, 4], F32)
        rs = sb("rsA", [128, 4], F32)
        rstd = sb("rstdA", [128, 4], F32)
        NST = 8
        stg = [sb(f"stgA{i}", [128, 512], BF16) for i in range(NST)]
        stgf = [sb(f"stgAf{i}", [128, 512], F32) for i in range(2)]
        sig = [sb(f"sigA{i}", [128, 512], F32) for i in range(2)]

        wtok = []
        for dc in range(8):
            wtok.append(pool.dma(w_in_sb[:, dc, :], dr["w_in"][l, dc * 128:(dc + 1) * 128, :], k.sem(f"w_in{dc % 4}")))

        xs_sem = [k.sem(f"A_xs{i}") for i in range(2)]
        xs_free = [[], []]
        ld_tok = {}

        def issue_loads(b):
            s = b % 2
            t0 = b * TB
            sp.wait(xs_free[s])
            ld_tok[b] = sp.dma(xs[s][:, :, :], xin[t0:t0 + TB, :].rearrange("(t p) d -> p t d", p=128), xs_sem[s])

        accT = Rot([k.ps[0], k.ps[1]])
        accM = Rot([k.ps[2], k.ps[3], k.ps[4], k.ps[5]])
        stg_r = Rot(stg)
        stg_sems = [k.sem(f"A_st{i}") for i in range(NST)]
        stgf_r = Rot(stgf)
        sig_r = Rot(sig)
        hs_free = []
        hT_free = []
        g1 = dr["g1_sb"][:, l * 8:(l + 1) * 8]
        lbt = sb("lbA", [128, 512], F32)
        omlt = sb("omlA", [128, 512], F32)
        sp.dma(lbt[:, :], dr["lbrep"][:, l, :], k.sem("consts"))
        tok_lb = sp.dma(omlt[:, :], dr["omlrep"][:, l, :], k.sem("consts"))
        lb, oml = lbt[:, :], omlt[:, :]

        def store_bf(eng_tok_fn, dst):
            si, st, fr = stg_r.get()
            t = eng_tok_fn(st, fr)
            sp.wait(t)
            td = sp.dma(dst, st[:, :], stg_sems[si])
            stg_r.release(si, td)

        issue_loads(0)
        for b in range(NB):
            s = b % 2
            t0 = b * TB
            if b + 1 < NB:
                issue_loads(b + 1)
            tok_x = ld_tok[b]
            tok_hs = []
            for tt in range(4):
                act.wait(tok_x)
                t = act.mark(act.h.activation(out=junk[:, :], in_=xs[s][:, tt, :], func=AF.Square,
                                              accum_out=ss[:, tt:tt + 1]))
                act.wait(t)
                t = act.mark(act.h.activation(out=rs[:, tt:tt + 1], in_=ss[:, tt:tt + 1], func=AF.Sqrt,
                                              scale=1.0 / D, bias=dr["eps_sb"][:, 0:1]))
                dve.wait(t)
                t = dve.mark(dve.h.reciprocal(out=rstd[:, tt:tt + 1], in_=rs[:, tt:tt + 1]))
                act.wait(t, hs_free)
                t = act.mark(act.h.activation(out=hs[:, tt, :], in_=xs[s][:, tt, :], func=AF.Copy,
                                              scale=rstd[:, tt:tt + 1]))
                tok_hs.append(t)
            xs_free[s] = [tok_hs[-1]]
            tok_hT = []
            hs_reads = []
            for tt in range(4):
                bi, bank, fr = accT.get()
                bv = bank[:, :].bitcast(BF16)
                pe.wait(tok_hs[tt], fr)
                for c in range(8):
                    ins = pe.h.transpose(out=bv[:, c * 128:(c + 1) * 128], in_=hs[:, tt, c * 128:(c + 1) * 128],
                                         identity=dr["ident"][:, :])
                tmm = pe.mark(ins)
                hs_reads.append(tmm)
                dve.wait(tmm, hT_free)
                t = dve.mark(dve.h.tensor_tensor(out=hT[:, :, tt * 128:(tt + 1) * 128],
                                                 in0=bv.rearrange("p (c t) -> p c t", c=8),
                                                 in1=bcast_mid(g1, 128), op=ALU.mult))
                accT.release(bi, t)
                tok_hT.append(t)
            hs_free = hs_reads
            last_mm = None
            fm = [("qT", 0, False), ("kT", 512, False), ("rqT", 1536, True), ("rgT", 3072, True)]
            for name, cbase, silu in fm:
                for h in range(4):
                    bi, bank, fr = accM.get()
                    pe.wait(tok_hT, wtok, fr)
                    c0 = cbase + h * 128
                    for c in range(8):
                        ins = pe.h.matmul(bank[:, :], lhsT=w_in_sb[:, c, c0:c0 + 128], rhs=hT[:, c, :],
                                          start=(c == 0), stop=(c == 7))
                    tmm = pe.mark(ins)
                    last_mm = tmm

                    def fill(st, fr2, bank=bank, tmm=tmm, silu=silu, bi=bi):
                        if silu:
                            act.wait(tmm, fr2)
                            t = act.mark(act.h.activation(out=st[:, :], in_=bank[:, :], func=AF.Silu))
                        else:
                            dve.wait(tmm, fr2)
                            t = dve.mark(dve.h.tensor_copy(out=st[:, :], in_=bank[:, :]))
                        accM.release(bi, t)
                        return t
                    store_bf(fill, dr[name][h, :, t0:t0 + TB])
            for tt in range(4):
                r0 = t0 + tt * 128
                for name, cbase in (("v", 1024), ("ri", 2560)):
                    bi, bank, fr = accM.get()
                    pe.wait(tok_hT, wtok, fr)
                    for c in range(8):
                        ins = pe.h.matmul(bank[:, :], lhsT=hT[:, c, tt * 128:(tt + 1) * 128],
                                          rhs=w_in_sb[:, c, cbase:cbase + 512], start=(c == 0), stop=(c == 7))
                    tmm = pe.mark(ins)
                    last_mm = tmm

                    def fill(st, fr2, bank=bank, tmm=tmm, bi=bi):
                        dve.wait(tmm, fr2)
                        t = dve.mark(dve.h.tensor_copy(out=st[:, :], in_=bank[:, :]))
                        accM.release(bi, t)
                        return t
                    store_bf(fill, dr[name][r0:r0 + 128, :])
                bi, bank, fr = accM.get()
                pe.wait(tok_hT, wtok, fr)
                for c in range(8):
                    ins = pe.h.matmul(bank[:, :], lhsT=hT[:, c, tt * 128:(tt + 1) * 128],
                                      rhs=w_in_sb[:, c, 2048:2560], start=(c == 0), stop=(c == 7))
                tmm = pe.mark(ins)
                last_mm = tmm
                gi, sg, gfr = sig_r.get()
                act.wait(tmm, gfr)
                t = act.mark(act.h.activation(out=sg[:, :], in_=bank[:, :], func=AF.Sigmoid))
                accM.release(bi, t)
                dve.wait(t, tok_lb)
                t = dve.mark(dve.h.tensor_tensor(out=sg[:, :], in0=sg[:, :], in1=oml, op=ALU.mult))
                dve.wait(t)
                tf = dve.mark(dve.h.tensor_tensor(out=sg[:, :], in0=sg[:, :], in1=lb, op=ALU.add))
                def fillk(st, fr2, sg=sg, tf=tf):
                    dve.wait(tf, fr2)
                    return dve.mark(dve.h.tensor_scalar(out=st[:, :], in0=sg[:, :], scalar1=-1.0, scalar2=1.0,
                                                        op0=ALU.mult, op1=ALU.add))
                si, st, sfr = stg_r.get()
                tk = fillk(st, sfr)
                sp.wait(tk)
                td = sp.dma(dr["kk"][r0:r0 + 128, :], st[:, :], stg_sems[si])
                stg_r.release(si, td)
                fi, sf, ffr = stgf_r.get()
                act.wait(tf, ffr)
                tg = act.mark(act.h.activation(out=sf[:, :], in_=sg[:, :], func=AF.Ln))
                sig_r.release(gi, tg, tk)
                si, st, sfr = stg_r.get()
                dve.wait(tg, sfr)
                thi = dve.mark(dve.h.tensor_copy(out=st[:, :], in_=sf[:, :]))
                si2, st2, sfr2 = stg_r.get()
                dve.wait(thi, sfr2)
                tlo = dve.mark(dve.h.tensor_tensor(out=st2[:, :], in0=sf[:, :], in1=st[:, :], op=ALU.subtract))
                stgf_r.release(fi, tlo)
                sp.wait(tlo)
                td = sp.dma(dr["ghi"][r0:r0 + 128, :], st[:, :], stg_sems[si])
                stg_r.release(si, td)
                td = sp.dma(dr["glo"][r0:r0 + 128, :], st2[:, :], stg_sems[si2])
                stg_r.release(si2, td)
            hT_free = [last_mm]
        k.barrier()


def phase_B(k, l, dr):
    nc = k.nc
    pe, act, dve, pool, sp = k.pe, k.act, k.dve, k.pool, k.sp
    QC = 512
    NQ = S // QC
    qT, kT, v, mixT = dr["qT"], dr["kT"], dr["v"], dr["mixT"]
    with ExitStack() as es:
        def sb(name, shape, dt):
            return es.enter_context(nc.sbuf_tensor(f"{name}_L{l}", shape, dt))
        qp1 = [sb(f"qp1_{i}", [128, S], BF16) for i in range(2)]
        qp2 = [sb(f"qp2_{i}", [128, S], BF16) for i in range(2)]
        kTs = [sb(f"kTs{i}", [128, S], BF16) for i in range(2)]
        Vs = [sb(f"Vs{i}", [128, 32, 128], BF16) for i in range(2)]
        P = [sb(f"P{i}", [128, 2, QC], BF16) for i in range(2)]
        r1 = sb("Br1", [128, QC], F32)
        r2 = sb("Br2", [128, QC], F32)
        ta = sb("Bta", [128, QC], F32)
        tb = sb("Btb", [128, QC], F32)
        to = sb("Bto", [128, QC], F32)
        tsq = sb("Btsq", [128, QC], BF16)
        trs = sb("Btrs", [128, QC], F32)
        ystg = [sb(f"Bys{i}", [128, QC], BF16) for i in range(2)]

        zt = []
        for i in range(2):
            zt.append(pool.mark(pool.h.memset(qp1[i][64:128, :], 0.0)))
            zt.append(pool.mark(pool.h.memset(qp2[i][0:64, :], 0.0)))

        hd_sem = [k.sem(f"B_hd{i}") for i in range(2)]
        hd_free = [[], []]
        hd_tok = {}

        def issue_head_loads(h):
            s = h % 2
            sp.wait(hd_free[s])
            sp.dma(kTs[s][:, :], kT[h, :, :], hd_sem[s])
            sp.dma(qp1[s][0:64, :], qT[h, 0:64, :], hd_sem[s])
            sp.dma(qp2[s][64:128, :], qT[h, 64:128, :], hd_sem[s])
            hd_tok[h] = sp.dma(Vs[s][:, :, :], v.rearrange("(t p) c -> p t c", p=128)[:, :, h * 128:(h + 1) * 128],
                               hd_sem[s])

        accS = Rot([(k.ps[4], k.ps[5]), (k.ps[6], k.ps[7])])
        S2b = [k.psum[:, (4 + 2 * i) * 512:(6 + 2 * i) * 512].rearrange("p (m c) -> p m c", m=2) for i in range(2)]
        O1, O2, s1, s2 = k.ps[0], k.ps[1], k.ps[2], k.ps[3]
        Prot = Rot(P)
        ysr = Rot(ystg)
        ys_sems = [k.sem(f"B_ys{i}") for i in range(2)]
        ident = dr["ident"]
        ones_bf = dr["ones_bf"]
        ones_f = dr["ones_f"]
        state = {"O_free": [], "fin_free": []}
        deferred = []

        def emit_S(h, qc, kt):
            s = h % 2
            q0 = qc * QC
            c0 = max(0, kt * 128 - q0)
            bi, (S1, S2), fr = accS.get()
            pe.wait(hd_tok[h], zt, fr)
            near = []
            if kt * 128 >= q0:
                near.append((c0, 0))
            if q0 <= kt * 128 + 128 < q0 + QC:
                near.append((kt * 128 + 128 - q0, 128))
            for Sb, qp in ((S1, qp1[s]), (S2, qp2[s])):
                ins = pe.h.matmul(Sb[:, c0:QC], lhsT=kTs[s][:, kt * 128:(kt + 1) * 128], rhs=qp[:, q0 + c0:q0 + QC],
                                  start=True, stop=(len(near) == 0))
                for i, (cs, go) in enumerate(near):
                    ins = pe.h.matmul(Sb[:, cs:cs + 128], lhsT=ident[:, :], rhs=dr["G_sb"][:, h, go:go + 128],
                                      start=False, stop=(i == len(near) - 1))
            return bi, S1, S2, pe.mark(ins), c0

        def emit_exp(h, sinfo):
            bi, S1, S2, tmm, c0 = sinfo
            pi, Pt, pfr = Prot.get()
            act.wait(tmm, pfr)
            t2 = act.mark(act.h.activation(out=Pt[:, :, c0:QC], in_=S2b[bi][:, :, c0:QC], func=AF.Exp, scale=0.125,
                                           bias=dr["cb_sb"][:, h:h + 1]))
            accS.release(bi, t2)
            return pi, Pt[:, 0, :], Pt[:, 1, :], t2, c0

        def emit_PV(h, kt, nkt, pinfo):
            s = h % 2
            pi, P1, P2, texp, c0 = pinfo
            pe.wait(texp)
            if kt == 0:
                pe.wait(state["O_free"])
            first, last = (kt == 0), (kt == nkt - 1)
            for Ob, sbk, Pm in ((O1, s1, P1), (O2, s2, P2)):
                pe.h.matmul(Ob[:, c0:QC], lhsT=Vs[s][:, kt, :], rhs=Pm[:, c0:QC], start=first, stop=last)
                ins = pe.h.matmul(sbk[:, c0:QC], lhsT=ones_bf[:, :], rhs=Pm[:, c0:QC], start=first, stop=last)
            t = pe.mark(ins)
            Prot.release(pi, t)
            return t

        def emit_finalize(h, qc, tlast):
            q0 = qc * QC
            dve.wait(tlast, state["fin_free"])
            dve.h.tensor_copy(out=r1[:, :], in_=s1[:, :])
            dve.h.tensor_copy(out=r2[:, :], in_=s2[:, :])
            dve.h.tensor_copy(out=ta[:, :], in_=O1[:, :])
            t = dve.mark(dve.h.tensor_copy(out=tb[:, :], in_=O2[:, :]))
            state["O_free"] = [t]
            dve.wait(t)
            t = dve.mark(dve.h.reciprocal(out=r1[:, :], in_=r1[:, :]))
            t = dve.mark(dve.h.reciprocal(out=r2[:, :], in_=r2[:, :]))
            dve.wait(t)
            t = dve.mark(dve.h.tensor_tensor(out=ta[:, :], in0=ta[:, :], in1=r1[:, :], op=ALU.mult))
            t = dve.mark(dve.h.tensor_tensor(out=tb[:, :], in0=tb[:, :], in1=r2[:, :], op=ALU.mult))
            dve.wait(t)
            to_tok = dve.mark(dve.h.scalar_tensor_tensor(out=to[:, :], in0=tb[:, :], scalar=dr["neglam_sb"][:, l:l + 1],
                                                         in1=ta[:, :], op0=ALU.mult, op1=ALU.add))

            sq_tok = {}

            def part1b():
                act.wait(to_tok)
                sq_tok["t"] = act.mark(act.h.activation(out=tsq[:, :], in_=to[:, :], func=AF.Square))

            def part2():
                tsq_tok = sq_tok["t"]
                bi = accS.i % len(accS.items)
                (B1, B2), fr = accS.items[bi], accS.free[bi]
                pe.wait(tsq_tok, fr)
                tm = pe.mark(pe.h.matmul(B1[:, :], lhsT=ones_bf[:, :], rhs=tsq[:, :], start=True, stop=True))
                act.wait(tm)
                t = act.mark(act.h.activation(out=trs[:, :], in_=B1[:, :], func=AF.Ln, scale=1.0 / 128,
                                              bias=dr["eps_sb"][:, 0:1]))
                accS.release(bi, t)
                act.wait(t)
                t = act.mark(act.h.activation(out=trs[:, :], in_=trs[:, :], func=AF.Exp, scale=-0.5))
                yi, ys, yfr = ysr.get()
                dve.wait(t, yfr)
                t = dve.mark(dve.h.scalar_tensor_tensor(out=ys[:, :], in0=to[:, :], scalar=dr["gAs_sb"][:, l:l + 1],
                                                        in1=trs[:, :], op0=ALU.mult, op1=ALU.mult))
                state["fin_free"] = [t]
                sp.wait(t)
                td = sp.dma(mixT[h * 128:(h + 1) * 128, q0:q0 + QC], ys[:, :], ys_sems[yi])
                ysr.release(yi, td)
            deferred.append([8, part1b])
            deferred.append([10, part2])

        def tick_deferred(force=False):
            for d in list(deferred):
                d[0] -= 1
                if d[0] <= 0 or force:
                    d[1]()
                    deferred.remove(d)

        issue_head_loads(0)
        for h in range(4):
            if h + 1 < 4:
                issue_head_loads(h + 1)
            pairs = [(qc, kt) for qc in reversed(range(NQ)) for kt in range(4 * qc + 4)]
            sinfo = emit_S(h, *pairs[0])
            for j, (qc, kt) in enumerate(pairs):
                nxt = emit_S(h, *pairs[j + 1]) if j + 1 < len(pairs) else None
                pinfo = emit_exp(h, sinfo)
                nkt = 4 * qc + 4
                tpv = emit_PV(h, kt, nkt, pinfo)
                tick_deferred()
                if kt == nkt - 1:
                    tick_deferred(force=True)
                    emit_finalize(h, qc, tpv)
                sinfo = nxt
            tick_deferred(force=True)
            hd_free[h % 2] = [tpv]
        k.barrier()


def phase_C(k, l, dr):
    nc = k.nc
    pe, act, dve, pool, sp = k.pe, k.act, k.dve, k.pool, k.sp
    NT = S // 128
    mixT = dr["mixT"]
    with ExitStack() as es:
        def sb(name, shape, dt):
            return es.enter_context(nc.sbuf_tensor(f"{name}_L{l}", shape, dt))
        gt = [sb(f"Cg{i}", [128, 2, 512], BF16) for i in range(2)]
        kkt = [sb(f"Ckk{i}", [128, 512], BF16) for i in range(2)]
        Vt = [sb(f"CV{i}", [128, 512], BF16) for i in range(2)]
        rqt = [sb(f"Crq{i}", [128, 4, 128], BF16) for i in range(2)]
        rgt = [sb(f"Crg{i}", [128, 4, 128], BF16) for i in range(2)]
        EA = sb("CEA", [128, 512], F32)
        EnA = sb("CEnA", [128, 512], F32)
        EH = sb("CEH", [128, 512], F32)
        ee = sb("Cee", [128, 4, 4], F32)
        qtl = sb("Cqtl", [128, 4, 128], BF16)
        ktl = sb("Cktl", [128, 4, 128], BF16)
        khb = sb("Ckhb", [128, 2, 512], BF16)
        attb = sb("Cattb", [128, 2, 4, 64], BF16)
        Smid = sb("CSmid", [128, 4, 128], BF16)
        stt = sb("Cstate", [128, 4, 128], F32)
        tmp = sb("Ctmp", [128, 4, 128], F32)
        sq = sb("Csq", [128, 512], BF16)
        rsn = sb("Crsn", [128, 512], F32)
        y1 = sb("Cy1", [128, 2, 4, 64], F32)
        ystg = [sb(f"Cys{i}", [128, 4, 128], BF16) for i in range(2)]

        init = [pool.mark(pool.h.memset(khb[:, :, :], 0.0)),
                pool.mark(pool.h.memset(attb[:, :, :, :], 0.0)),
                pool.mark(pool.h.memset(stt[:, :, :], 0.0))]
        AT, EB, MB, KB, ATT, OB, UB, NB = k.ps
        KBv = KB[:, :].bitcast(BF16)

        ld_sem = [k.sem(f"C_ld{i}") for i in range(2)]
        ld_free = [[], []]
        ld_tok = {}

        def issue_loads(tt):
            s = tt % 2
            t0 = tt * 128
            sp.wait(ld_free[s])
            sp.dma(gt[s][:, 0, :], dr["ghi"][t0:t0 + 128, :], ld_sem[s])
            sp.dma(gt[s][:, 1, :], dr["glo"][t0:t0 + 128, :], ld_sem[s])
            sp.dma(kkt[s][:, :], dr["kk"][t0:t0 + 128, :], ld_sem[s])
            sp.dma(Vt[s][:, :], dr["ri"][t0:t0 + 128, :], ld_sem[s])
            sp.dma(rqt[s][:, :, :], dr["rqT"][:, :, t0:t0 + 128].rearrange("h p t -> p h t"), ld_sem[s])
            ld_tok[tt] = sp.dma(rgt[s][:, :, :], dr["rgT"][:, :, t0:t0 + 128].rearrange("h p t -> p h t"), ld_sem[s])

        ys_sems = [k.sem(f"C_ys{i}") for i in range(2)]
        ysr = Rot(ystg)
        rd = {}

        def R(name):
            return rd.get(name, [])

        M1, Em, M2, tri = dr["M1_sb"], dr["Em_sb"], dr["M2_sb"], dr["tri_sb"]
        tok_state = init[2]
        issue_loads(0)
        for tt in range(NT):
            s = tt % 2
            t0 = tt * 128
            if tt + 1 < NT:
                issue_loads(tt + 1)
            tl = ld_tok[tt]
            pe.wait(tl, R("AT"))
            for h in range(4):
                for j in range(2):
                    ins = pe.h.matmul(AT[:, h * 128:(h + 1) * 128], lhsT=gt[s][:, j, h * 128:(h + 1) * 128], rhs=M1[:, :],
                                      start=(j == 0), stop=(j == 1))
            t_AT = pe.mark(ins)
            pe.wait(R("EB"))
            for h in range(4):
                for j in range(2):
                    ins = pe.h.matmul(EB[:, h * 4:(h + 1) * 4], lhsT=gt[s][:, j, h * 128:(h + 1) * 128], rhs=Em[:, :],
                                      start=(j == 0), stop=(j == 1))
            t_EB = pe.mark(ins)
            pe.wait(R("MB"))
            pe.h.matmul(MB[:, :], lhsT=M2[:, :], rhs=gt[s][:, 0, :], start=True, stop=False)
            t_MB = pe.mark(pe.h.matmul(MB[:, :], lhsT=M2[:, :], rhs=gt[s][:, 1, :], start=False, stop=True))
            pe.wait(R("KB"))
            for h in range(4):
                ins = pe.h.transpose(out=KBv[:, h * 128:(h + 1) * 128], in_=kkt[s][:, h * 128:(h + 1) * 128],
                                     identity=dr["ident"][:, :])
            t_KB = pe.mark(ins)
            act.wait(t_AT, R("EA"))
            t_EA = act.mark(act.h.activation(out=EA[:, :], in_=AT[:, :], func=AF.Exp))
            act.wait(R("EnA"))
            t_EnA = act.mark(act.h.activation(out=EnA[:, :], in_=AT[:, :], func=AF.Exp, scale=-1.0))
            rd["AT"] = [t_EnA]
            act.wait(t_MB, R("EH"))
            t_EH = act.mark(act.h.activation(out=EH[:, :], in_=MB[:, :], func=AF.Exp))
            rd["MB"] = [t_EH]
            act.wait(t_EB, R("ee"))
            t_ee = act.mark(act.h.activation(out=ee[:, :, :], in_=EB[:, 0:16].rearrange("p (h j) -> p h j", h=4),
                                             func=AF.Exp))
            rd["EB"] = [t_ee]
            pool.wait(t_EA, tl, R("qtl"))
            t_qtl = pool.mark(pool.h.tensor_tensor(out=qtl[:, :, :], in0=rqt[s][:, :, :],
                                                   in1=EA[:, :].rearrange("p (h t) -> p h t", h=4), op=ALU.mult))
            rd["EA"] = [t_qtl]
            dve.wait(t_KB, t_EnA, R("ktl"))
            t_ktl = dve.mark(dve.h.tensor_tensor(out=ktl[:, :, :], in0=KBv[:, 0:512].rearrange("p (h t) -> p h t", h=4),
                                                 in1=EnA[:, :].rearrange("p (h t) -> p h t", h=4), op=ALU.mult))
            rd["KB"] = [t_ktl]
            rd["EnA"] = [t_ktl]
            pool.wait(t_EH, R("khb"), init[0])
            for c in range(2):
                t_khb = pool.mark(pool.h.tensor_tensor(out=khb[c * 64:(c + 1) * 64, c, :], in0=kkt[s][c * 64:(c + 1) * 64, :],
                                                       in1=EH[c * 64:(c + 1) * 64, :], op=ALU.mult))
            rd["EH"] = [t_khb]
            pe.wait(t_qtl, t_ktl, R("ATT"))
            for c in range(2):
                for h in range(4):
                    ins = pe.h.matmul(ATT[:, c * 256 + h * 64:c * 256 + (h + 1) * 64], lhsT=ktl[:, h, :],
                                      rhs=qtl[:, h, c * 64:(c + 1) * 64], start=True, stop=True)
            t_ATT = pe.mark(ins)
            rd["ktl"] = [t_ATT]
            dve.wait(t_ATT, R("attb"), init[1])
            for c in range(2):
                t_attb = dve.mark(dve.h.tensor_tensor(
                    out=attb[c * 64:(c + 1) * 64, c, :, :],
                    in0=ATT[c * 64:(c + 1) * 64, c * 256:(c + 1) * 256].rearrange("p (h t) -> p h t", h=4),
                    in1=tri[c * 64:(c + 1) * 64, :, :], op=ALU.mult))
            rd["ATT"] = [t_attb]
            eev = ee[:, :, :]
            for c in range(2):
                dve.wait(tok_state, t_ee, R("Smid"))
                t_Smid = dve.mark(dve.h.tensor_tensor(out=Smid[:, :, :], in0=stt[:, :, :],
                                                      in1=bcast_mid(eev[:, :, c], 128), op=ALU.mult))
                pe.wait(t_attb, t_Smid, tl, R("OB") if c == 0 else [])
                for h in range(4):
                    o_ap = OB[:, c * 256 + h * 64:c * 256 + (h + 1) * 64]
                    pe.h.matmul(o_ap, lhsT=Vt[s][:, h * 128:(h + 1) * 128], rhs=attb[:, c, h, :], start=True, stop=False)
                    ins = pe.h.matmul(o_ap, lhsT=Smid[:, h, :], rhs=qtl[:, h, c * 64:(c + 1) * 64], start=False, stop=True)
                t_OB = pe.mark(ins)
                rd["Smid"] = [t_OB]
                pe.wait(t_khb, R("UB"))
                for h in range(4):
                    ins = pe.h.matmul(UB[:, h * 128:(h + 1) * 128], lhsT=khb[:, c, h * 128:(h + 1) * 128],
                                      rhs=Vt[s][:, h * 128:(h + 1) * 128], start=True, stop=True)
                t_UB = pe.mark(ins)
                pool.wait(tok_state, t_ee, t_Smid, R("tmp"))
                t_tmp = pool.mark(pool.h.tensor_tensor(out=tmp[:, :, :], in0=stt[:, :, :],
                                                       in1=bcast_mid(eev[:, :, 2 + c], 128), op=ALU.mult))
                dve.wait(t_tmp, t_UB, t_Smid)
                tok_state = dve.mark(dve.h.tensor_tensor(out=stt[:, :, :], in0=tmp[:, :, :],
                                                         in1=UB[:, :].rearrange("p (h v) -> p h v", h=4), op=ALU.add))
                rd["UB"] = [tok_state]
                rd["tmp"] = [tok_state]
            rd["attb"] = [t_OB]
            rd["qtl"] = [t_OB]
            rd["khb"] = [t_UB]
            rd["ee"] = [tok_state]
            act.wait(t_OB, R("sq"))
            t_sq = act.mark(act.h.activation(out=sq[:, :], in_=OB[:, :], func=AF.Square))
            pe.wait(t_sq, R("NB"))
            t_NB = pe.mark(pe.h.matmul(NB[:, :], lhsT=dr["ones_bf"][:, :], rhs=sq[:, :], start=True, stop=True))
            rd["sq"] = [t_NB]
            act.wait(t_NB, R("rsn"))
            t = act.mark(act.h.activation(out=rsn[:, :], in_=NB[:, :], func=AF.Ln, scale=1.0 / 128,
                                          bias=dr["eps_sb"][:, 0:1]))
            rd["NB"] = [t]
            act.wait(t)
            t = act.mark(act.h.activation(out=rsn[:, :], in_=rsn[:, :], func=AF.Exp, scale=-0.5))
            dve.wait(t, R("y1"))
            t_y1 = dve.mark(dve.h.scalar_tensor_tensor(out=y1[:, :, :, :],
                                                       in0=OB[:, :].rearrange("p (c h t) -> p c h t", c=2, h=4),
                                                       scalar=dr["gH_sb"][:, l:l + 1],
                                                       in1=rsn[:, :].rearrange("p (c h t) -> p c h t", c=2, h=4),
                                                       op0=ALU.mult, op1=ALU.mult))
            rd["OB"] = [t_y1]
            rd["rsn"] = [t_y1]
            yi, ys, yfr = ysr.get()
            pool.wait(t_y1, yfr)
            t_ys = pool.mark(pool.h.tensor_tensor(out=ys[:, :, :].rearrange("p h (c t) -> p c h t", c=2),
                                                  in0=y1[:, :, :, :],
                                                  in1=rgt[s][:, :, :].rearrange("p h (c t) -> p c h t", c=2), op=ALU.mult))
            rd["y1"] = [t_ys]
            sp.wait(t_ys)
            td = sp.dma(mixT[512:1024, t0:t0 + 128].rearrange("(h p) t -> p h t", p=128), ys[:, :, :], ys_sems[yi])
            ysr.release(yi, td)
            ld_free[s] = [t_ys, t_UB, t_OB, t_KB, t_MB]
        k.barrier()


def t5_bucket_np(rel):
    n_half, max_exact = 16, 8
    ret = np.where(rel > 0, n_half, 0)
    n = np.abs(rel)
    nf = np.maximum(n, 1).astype(np.float32)
    large = max_exact + (np.log(nf / np.float32(max_exact)) / np.float32(math.log(128 / max_exact))
                         * np.float32(n_half - max_exact)).astype(np.int32)
    large = np.minimum(large, n_half - 1)
    return ret + np.where(n < max_exact, n, large)


LAM_INIT = [0.8 - 0.6 * math.exp(-0.3 * l) for l in range(DEPTH)]

CF = {}
_o = 0
for _n, _w in (("ones_f", 128), ("M1", 128), ("M2", 128), ("Em", 4), ("eps", 1), ("tri", 256), ("OH8", 384),
               ("maskadd", 128), ("oml_init", 4), ("neg_lam_init", 4)):
    CF[_n] = (_o, _w)
    _o += _w
CF_W = _o
CB = {"ident": (0, 128), "J": (128, 128), "ones_bf": (256, 128), "M1b": (384, 128), "M2b": (512, 128), "Emb": (640, 4)}
CB_W = 644


def host_consts():
    cf = np.zeros((128, CF_W), np.float32)

    def put(name, arr):
        o, w = CF[name]
        cf[:arr.shape[0], o:o + w] = arr.reshape(arr.shape[0], -1)

    put("ones_f", np.ones((128, 128), np.float32))
    tp = np.arange(128)[:, None]
    t = np.arange(128)[None, :]
    same = (tp // 64) == (t // 64)
    mid = (t // 64) * 64 + 31
    put("M1", (same & (tp <= t)).astype(np.float32) - (same & (tp <= mid)).astype(np.float32))
    put("M2", (same & (tp > t)).astype(np.float32))
    Em = np.zeros((128, 4), np.float32)
    ar = np.arange(128)
    for c in range(2):
        Em[:, c] = ((ar // 64 == c) & (ar <= c * 64 + 31)).astype(np.float32)
        Em[:, 2 + c] = (ar // 64 == c).astype(np.float32)
    put("Em", Em)
    put("eps", np.full((128, 1), EPS, np.float32))
    s_ = (ar % 64)[:, None, None]
    tq = np.arange(64)[None, None, :]
    put("tri", np.broadcast_to((s_ <= tq), (128, 4, 64)).astype(np.float32))
    i = np.arange(384)
    rel = 127 - i
    bk = t5_bucket_np(rel)
    oh = np.zeros((32, 384), np.float32)
    oh[bk, i] += 8.0
    oh[15, :] -= 8.0
    oh[:, 383] = 0.0
    put("OH8", oh)
    p = np.arange(128)[:, None]
    j = np.arange(128)[None, :]
    put("maskadd", np.where((p // 64) > (j // 64), -240000.0, 0.0).astype(np.float32))
    put("oml_init", np.broadcast_to(np.array([1.0 - x for x in LAM_INIT], np.float32)[None, :], (128, 4)))
    put("neg_lam_init", np.broadcast_to(np.array([-x for x in LAM_INIT], np.float32)[None, :], (128, 4)))
    cb = np.zeros((128, CB_W), np.float32)
    cb[:, 0:128] = np.eye(128)
    cb[:, 128:256] = np.eye(128)[::-1]
    cb[:, 256:384] = 1.0
    for nm in ("M1", "M2", "Em"):
        o, w = CF[nm]
        ob, wb = CB[nm + "b"]
        cb[:, ob:ob + wb] = cf[:, o:o + w]
    return cf, cb.astype(ml_dtypes.bfloat16)


def setup(k, dr, ext):
    nc = k.nc
    pe, act, dve, pool, sp = k.pe, k.act, k.dve, k.pool, k.sp
    cs = k.sem("consts")
    cf = nc.alloc_sbuf_tensor("cf_sb", [128, CF_W], F32)
    cbt = nc.alloc_sbuf_tensor("cb_sb16", [128, CB_W], BF16)
    toks = [sp.dma(cf[:, :], ext["cf"], cs), sp.dma(cbt[:, :], ext["cb16"], cs)]

    def fv(name):
        o, w = CF[name]
        return cf[:, o:o + w]
    dr["ones_f"] = fv("ones_f")
    dr["M1_sb"] = cbt[:, 384:512]
    dr["M2_sb"] = cbt[:, 512:640]
    dr["Em_sb"] = cbt[:, 640:644]
    dr["eps_sb"] = fv("eps")
    dr["tri_sb"] = fv("tri").rearrange("p (h t) -> p h t", h=4)
    dr["ident"] = cbt[:, 0:128]
    Jm = cbt[:, 128:256]
    dr["ones_bf"] = cbt[:, 256:384]
    for name, w in (("g1", 32), ("g2", 32), ("gA", 4), ("gH", 4)):
        t = nc.alloc_sbuf_tensor(name + "_sb", [128, w], F32)
        toks.append(sp.dma(t[:, :], ext[name], cs))
        dr[name + "_sb"] = t
    dr["final_g"] = ext["final_g"]
    neglam = nc.alloc_sbuf_tensor("neglam_sb", [128, 4], F32)
    gAs = nc.alloc_sbuf_tensor("gAs_sb", [128, 4], F32)
    cbias = nc.alloc_sbuf_tensor("cbias_sb", [128, 4], F32)
    G = nc.alloc_sbuf_tensor("G_sb", [128, 4, 256], BF16)
    dr["neglam_sb"], dr["gAs_sb"], dr["cb_sb"], dr["G_sb"] = neglam, gAs, cbias, G
    toks.append(sp.dma(cbias[:, :], ext["rel_bias"][15:16, :].partition_broadcast(128), cs))
    with ExitStack() as es:
        def sb(name, shape, dt):
            return es.enter_context(nc.sbuf_tensor(name, shape, dt))
        lg = sb("S_lg", [128, 4, 512], F32)
        lb = sb("S_lb", [128, 4, 512], F32)
        oml = sb("S_oml", [128, 4, 512], F32)
        ssum = sb("S_sum", [128, 512], F32)
        lq = sb("S_lq", [128, 4, 4, 64], F32)
        pr = sb("S_pr", [128, 4, 2, 64], F32)
        dd = sb("S_dd", [128, 4, 2], F32)
        rb = sb("S_rb", [32, 4], F32)
        tv = sb("S_tv", [4, 384], F32)
        tvb = sb("S_tvb", [4, 384], BF16)
        Hk = sb("S_H", [128, 4, 256], BF16)
        toks.append(sp.dma(lg[:, :, :], ext["lb_logits"].partition_broadcast(128), cs))
        toks.append(sp.dma(lq[:, :, :, :], ext["lam_qk"].partition_broadcast(128), cs))
        toks.append(sp.dma(rb[:, :], ext["rel_bias"], cs))
        for e in k.engs:
            e.wait(toks[-1])
        t = act.mark(act.h.activation(out=lg[:, :, :], in_=lg[:, :, :], func=AF.Exp))
        dve.wait(t)
        t = dve.mark(dve.h.tensor_tensor(out=ssum[:, :], in0=lg[:, 0, :], in1=lg[:, 1, :], op=ALU.add))
        dve.wait(t)
        t = dve.mark(dve.h.tensor_tensor(out=ssum[:, :], in0=ssum[:, :], in1=lg[:, 2, :], op=ALU.add))
        dve.wait(t)
        t = dve.mark(dve.h.tensor_tensor(out=ssum[:, :], in0=ssum[:, :], in1=lg[:, 3, :], op=ALU.add))
        dve.wait(t)
        t = dve.mark(dve.h.reciprocal(out=ssum[:, :], in_=ssum[:, :]))
        dve.wait(t)
        t = dve.mark(dve.h.memset(lb[:, 0, :], 0.0))
        t = dve.mark(dve.h.tensor_tensor(out=lb[:, 1, :], in0=lg[:, 1, :], in1=ssum[:, :], op=ALU.mult))
        t = dve.mark(dve.h.tensor_tensor(out=lb[:, 2, :], in0=lg[:, 2, :], in1=ssum[:, :], op=ALU.mult))
        t = dve.mark(dve.h.tensor_tensor(out=lb[:, 3, :], in0=lg[:, 3, :], in1=ssum[:, :], op=ALU.mult))
        dve.wait(t)
        t = dve.mark(dve.h.tensor_tensor(out=lb[:, 2, :], in0=lb[:, 2, :], in1=lb[:, 1, :], op=ALU.add))
        dve.wait(t)
        t = dve.mark(dve.h.tensor_tensor(out=lb[:, 3, :], in0=lb[:, 3, :], in1=lb[:, 2, :], op=ALU.add))
        dve.wait(t)
        t = dve.mark(dve.h.tensor_scalar(out=oml[:, :, :], in0=lb[:, :, :], scalar1=-1.0, scalar2=1.0,
                                         op0=ALU.mult, op1=ALU.add))
        sp.wait(t)
        sp.dma(dr["lbrep"], lb[:, :, :], cs)
        sp.dma(dr["omlrep"], oml[:, :, :], cs)
        t = dve.mark(dve.h.tensor_tensor(out=pr[:, :, 0, :], in0=lq[:, :, 0, :], in1=lq[:, :, 1, :], op=ALU.mult))
        t = dve.mark(dve.h.tensor_tensor(out=pr[:, :, 1, :], in0=lq[:, :, 2, :], in1=lq[:, :, 3, :], op=ALU.mult))
        dve.wait(t)
        t = dve.mark(dve.h.reduce_sum(out=dd[:, :, :], in_=pr[:, :, :, :], axis=AX.X))
        act.wait(t)
        t = act.mark(act.h.activation(out=dd[:, :, :], in_=dd[:, :, :], func=AF.Exp))
        dve.wait(t)
        t = dve.mark(dve.h.tensor_tensor(out=neglam[:, :], in0=dd[:, :, 1], in1=dd[:, :, 0], op=ALU.subtract))
        dve.wait(t)
        t = dve.mark(dve.h.tensor_tensor(out=neglam[:, :], in0=neglam[:, :], in1=fv("neg_lam_init"), op=ALU.add))
        t = dve.mark(dve.h.tensor_tensor(out=gAs[:, :], in0=dr["gA_sb"][:, :], in1=fv("oml_init"), op=ALU.mult))
        o8, w8 = CF["OH8"]
        tm = pe.mark(pe.h.matmul(k.ps[0][0:4, 0:384], lhsT=rb[:, :], rhs=cf[0:32, o8:o8 + w8], start=True, stop=True))
        dve.wait(tm)
        t = dve.mark(dve.h.tensor_copy(out=tvb[:, :], in_=k.ps[0][0:4, 0:384]))
        sp.wait(t)
        scr = dr["tvec"]
        t = sp.dma(scr, tvb[:, :], cs)
        sp.wait(t)
        hank = bass.AP(tensor=scr.tensor, offset=scr.offset, ap=[[1, 128], [384, 4], [1, 256]])
        t = sp.dma(Hk[:, :, :], hank, cs)
        pe.wait(t)
        pe.h.matmul(k.ps[1][:, :], lhsT=Jm, rhs=Hk[:, 0:2, :], start=True, stop=True)
        tm = pe.mark(pe.h.matmul(k.ps[2][:, :], lhsT=Jm, rhs=Hk[:, 2:4, :], start=True, stop=True))
        dve.wait(tm)
        for hh, bank in ((0, k.ps[1]), (1, k.ps[2])):
            bv = bank[:, :].rearrange("p (h j) -> p h j", h=2)
            t = dve.mark(dve.h.tensor_tensor(out=G[:, 2 * hh:2 * hh + 2, 0:128], in0=bv[:, :, 0:128],
                                             in1=fv("maskadd").unsqueeze(1).broadcast_to([128, 2, 128]), op=ALU.add))
            t = dve.mark(dve.h.tensor_copy(out=G[:, 2 * hh:2 * hh + 2, 128:256], in_=bv[:, :, 128:256]))
        k.barrier()


def build_full(nlayers=DEPTH, debug=False):
    nc = bass.Bass("TRN2", target_bir_lowering=False)
    k = K(nc)

    def din(name, shape, dt=F32):
        return nc.dram_tensor(name, shape, dt, kind="ExternalInput").ap()

    def dscr(name, shape, dt):
        return nc.dram_tensor(name, shape, dt, kind="Internal").ap()

    ext = {
        "x": din("x", [S, D]),
        "w_in": din("w_in", [DEPTH, D, INC]),
        "w_out": din("w_out", [DEPTH, D, D]),
        "w_up": din("w_up", [DEPTH, D, DFF]),
        "w_down": din("w_down", [DEPTH, DFF, D]),
        "lb_logits": din("lb_logits", [DEPTH, 512]),
        "lam_qk": din("lam_qk", [DEPTH, 4, 64]),
        "rel_bias": din("rel_bias", [32, 4]),
        "final_g": din("final_g", [D]),
        "g1": din("g1", [128, 32]),
        "g2": din("g2", [128, 32]),
        "gA": din("gA", [128, 4]),
        "gH": din("gH", [128, 4]),
        "cf": din("cf", [128, CF_W]),
        "cb16": din("cb16", [128, CB_W], BF16),
    }
    y = nc.dram_tensor("y", [S, D], F32, kind="ExternalOutput").ap()
    dr = {"y": y}
    for n in ("w_in", "w_out", "w_up", "w_down"):
        dr[n] = ext[n]
    xres = dscr("xres", [S, D], F32)
    for n in ("qT", "kT", "rqT", "rgT"):
        dr[n] = dscr(n, [4, 128, S], BF16)
    for n in ("v", "ri", "kk"):
        dr[n] = dscr(n, [S, 512], BF16)
    dr["ghi"] = dscr("ghi", [S, 512], BF16)
    dr["glo"] = dscr("glo", [S, 512], BF16)
    dr["mixT"] = dscr("mixT", [1024, S], BF16)
    dr["tvec"] = dscr("tvec", [4, 384], BF16)
    dr["lbrep"] = dscr("lbrep", [128, DEPTH, 512], F32)
    dr["omlrep"] = dscr("omlrep", [128, DEPTH, 512], F32)
    setup(k, dr, ext)
    for l in range(nlayers):
        dr["x_in"] = ext["x"] if l == 0 else xres
        dr["x_out"] = xres
        phase_A(k, l, dr)
        phase_B(k, l, dr)
        phase_C(k, l, dr)
        phase_D(k, l, dr, final=(l == nlayers - 1))
    k.barrier()
    return nc


def make_in_maps(inputs):
    cf, cb16 = host_consts()
    f32 = lambda a: np.ascontiguousarray(np.asarray(a, dtype=np.float32))
    shared = {
        "w_in": f32(inputs["w_in"]), "w_out": f32(inputs["w_out"]),
        "w_up": f32(inputs["w_up"]), "w_down": f32(inputs["w_down"]),
        "lb_logits": f32(inputs["lb_logits"]), "lam_qk": f32(inputs["lam_qk"]),
        "rel_bias": f32(inputs["rel_bias"]), "final_g": f32(inputs["final_g"]),
        "g1": f32(np.asarray(inputs["norm1_g"]).reshape(DEPTH, 8, 128).transpose(2, 0, 1).reshape(128, 32)),
        "g2": f32(np.asarray(inputs["norm2_g"]).reshape(DEPTH, 8, 128).transpose(2, 0, 1).reshape(128, 32)),
        "gA": f32(np.asarray(inputs["attn_norm_g"]).T),
        "gH": f32(np.asarray(inputs["hgrn_norm_g"]).T),
        "cf": cf, "cb16": cb16,
    }
    x = f32(inputs["x"])
    return [dict(shared, x=x[b]) for b in range(x.shape[0])]


_NC_CACHE = {}


def kernel(**inputs):
    if "nc" not in _NC_CACHE:
        _NC_CACHE["nc"] = build_full()
    nc = _NC_CACHE["nc"]
    in_maps = make_in_maps(inputs)
    res = run_bass_kernel_spmd(nc, in_maps, core_ids=list(range(NCORES)))
    return np.stack([np.asarray(r["y"], dtype=np.float32) for r in res.results], axis=0)
```

```python
import math
from contextlib import ExitStack

import numpy as np
import ml_dtypes
import concourse.bass as bass
import concourse.mybir as mybir
from concourse.bass_utils import run_bass_kernel_spmd

F32 = mybir.dt.float32
BF16 = mybir.dt.bfloat16
AF = mybir.ActivationFunctionType
ALU = mybir.AluOpType
AX = mybir.AxisListType

S = 4096
D = 1024
DEPTH = 4
DFF = 4096
INC = 3584
EPS = 1e-6
NCORES = 8


class Sem:
    def __init__(self, nc, name):
        self.h = nc.alloc_semaphore(name)
        self.cnt = 0
        self.name = name


class Eng:
    def __init__(self, k, h, name):
        self.k = k
        self.h = h
        self.sem = Sem(k.nc, "e_" + name)
        self.seen = {}

    def wait(self, *toks):
        for t in toks:
            if t is None:
                continue
            if isinstance(t, list):
                self.wait(*t)
                continue
            sem, val = t
            if self.seen.get(sem.name, 0) >= val:
                continue
            self.h.wait_ge(sem.h, val)
            self.seen[sem.name] = val

    def mark(self, inst):
        self.sem.cnt += 1
        inst.then_inc(self.sem.h, 1)
        tok = (self.sem, self.sem.cnt)
        self.k.latest[self.sem.name] = tok
        return tok

    def dma(self, out, in_, sem):
        inst = self.h.dma_start(out=out, in_=in_)
        sem.cnt += 16
        inst.then_inc(sem.h, 16)
        tok = (sem, sem.cnt)
        self.k.latest[sem.name] = tok
        return tok


class Rot:
    def __init__(self, items):
        self.items = items
        self.i = 0
        self.free = [[] for _ in items]

    def get(self):
        idx = self.i % len(self.items)
        self.i += 1
        return idx, self.items[idx], self.free[idx]

    def release(self, idx, *toks):
        self.free[idx] = list(toks)


class K:
    def __init__(self, nc):
        self.nc = nc
        self.latest = {}
        self.pe = Eng(self, nc.tensor, "pe")
        self.act = Eng(self, nc.scalar, "act")
        self.dve = Eng(self, nc.vector, "dve")
        self.pool = Eng(self, nc.gpsimd, "pool")
        self.sp = Eng(self, nc.sync, "sp")
        self.engs = [self.pe, self.act, self.dve, self.pool, self.sp]
        self.sems = {}
        self.psum = nc.alloc_psum_tensor("psall", [128, 4096], F32)
        self.ps = [self.psum[:, i * 512:(i + 1) * 512] for i in range(8)]

    def sem(self, name):
        if name not in self.sems:
            self.sems[name] = Sem(self.nc, name)
        return self.sems[name]

    def barrier(self):
        toks = list(self.latest.values())
        for e in self.engs:
            e.wait(*toks)


def bcast_mid(ap2d, n):
    p, a = ap2d.shape
    return ap2d.unsqueeze(2).broadcast_to([p, a, n])


def alloc_D_weights(k, l, dr, es):
    nc = k.nc
    pool = k.pool
    pre = {}
    pre["w_out"] = es.enter_context(nc.sbuf_tensor(f"w_out_sb_L{l}", [128, 8, 1024], BF16))
    pre["w_up"] = es.enter_context(nc.sbuf_tensor(f"w_up_sb_L{l}", [128, 8, 4096], BF16))
    pre["w_dn"] = es.enter_context(nc.sbuf_tensor(f"w_dn_sb_L{l}", [128, 32, 1024], BF16))
    pre["t_out"], pre["t_up"], pre["t_dn"] = [], [], []
    thunks = []

    def mk(lst, out, in_, name):
        def f():
            lst.append(pool.dma(out, in_, k.sem(name)))
        return f
    thunks.append(mk(pre["t_out"], pre["w_out"][:, :, :], dr["w_out"][l].rearrange("(c p) d -> p c d", p=128), "w_out"))
    for dc in range(8):
        thunks.append(mk(pre["t_up"], pre["w_up"][:, dc, :], dr["w_up"][l, dc * 128:(dc + 1) * 128, :], f"w_up{dc}"))
    for j in range(4):
        thunks.append(mk(pre["t_dn"], pre["w_dn"][:, j * 8:(j + 1) * 8, :],
                         dr["w_down"][l, j * 1024:(j + 1) * 1024, :].rearrange("(c p) d -> p c d", p=128), f"w_dn{j}"))
    return pre, thunks


def phase_D(k, l, dr, final, pre):
    nc = k.nc
    pe, act, dve, pool, sp = k.pe, k.act, k.dve, k.pool, k.sp
    TB = 256
    NB = S // TB
    xin = dr["x_in"]
    xout = dr["x_out"]
    mixT = dr["mixT"]
    with ExitStack() as es:
        def sb(name, shape, dt):
            return es.enter_context(nc.sbuf_tensor(f"{name}_L{l}", shape, dt))
        w_out_sb, w_up_sb, w_dn_sb = pre["w_out"], pre["w_up"], pre["w_dn"]
        wtok_out, wtok_up, wtok_dn = pre["t_out"], pre["t_up"], pre["t_dn"]
        mT = [sb("mT0", [128, 8, TB], BF16)]
        xs = [sb(f"xs{i}", [128, 2, 1024], F32) for i in range(2)]
        h2 = sb("h2", [128, 2, 1024], BF16)
        h2T = sb("h2T", [128, 8, TB], BF16)
        u2T = sb("u2T", [128, 32, TB], BF16)
        sq = [sb(f"sq{i}", [128, TB], F32) for i in range(2)]
        ss = sb("ssD", [128, 4], F32)
        rs = sb("rsD", [128, 4], F32)
        rstd = sb("rstdD", [128, 4], F32)
        if final:
            gF = sb("gF_sb", [128, 1024], F32)
            fjunk = sb("fjunkD", [128, 1024], BF16)
            tok_gF = sp.dma(gF[:, :], dr["final_g"].partition_broadcast(128), k.sem("consts"))

        mT_sem = [k.sem("D_mT0")]
        xs_sem = [k.sem(f"D_xs{i}") for i in range(2)]
        st_sem = [k.sem(f"D_st{i}") for i in range(2)]
        mT_free = [[]]
        xs_free = [[], []]
        ld_tok = {}
        ldm_tok = {}

        def issue_mT(b):
            t0 = b * TB
            sp.wait(mT_free[0])
            ldm_tok[b] = sp.dma(mT[0][:, :, :], mixT.rearrange("(c p) t -> p c t", p=128)[:, :, t0:t0 + TB], mT_sem[0])

        def issue_loads(b):
            s = b % 2
            t0 = b * TB
            sp.wait(xs_free[s])
            ld_tok[b] = sp.dma(xs[s][:, :, :], xin[t0:t0 + TB, :].rearrange("(t p) d -> p t d", p=128), xs_sem[s])

        accA = Rot([k.ps[0], k.ps[1]])
        accU = Rot([k.ps[2], k.ps[3]])
        accT = Rot([k.ps[4], k.ps[5]])
        sqr = Rot(sq)
        g2 = dr["g2_sb"][:, l * 8:(l + 1) * 8]

        def do_outproj_norm(b):
            s = b % 2
            tok_mT, tok_xs = ldm_tok[b], ld_tok[b]
            x1 = [[None, None], [None, None]]
            mm_reads = []
            for tt in range(2):
                for hf in range(2):
                    bi, bank, fr = accA.get()
                    pe.wait(tok_mT, wtok_out, fr)
                    for c in range(8):
                        ins = pe.h.matmul(bank[:, :], lhsT=mT[0][:, c, tt * 128:(tt + 1) * 128],
                                          rhs=w_out_sb[:, c, hf * 512:(hf + 1) * 512], start=(c == 0), stop=(c == 7))
                    tmm = pe.mark(ins)
                    mm_reads.append(tmm)
                    dve.wait(tmm, tok_xs)
                    t = dve.mark(dve.h.tensor_tensor(out=xs[s][:, tt, hf * 512:(hf + 1) * 512], in0=bank[:, :],
                                                     in1=xs[s][:, tt, hf * 512:(hf + 1) * 512], op=ALU.add))
                    accA.release(bi, t)
                    x1[tt][hf] = t
            mT_free[0] = mm_reads
            if b + 1 < NB:
                issue_mT(b + 1)
            tok_h2 = []
            for tt in range(2):
                act.wait(x1[tt][0], x1[tt][1], st_["h2_free"])
                t = act.mark(act.h.activation(out=h2[:, tt, :], in_=xs[s][:, tt, :], func=AF.Square,
                                              accum_out=ss[:, tt:tt + 1]))
                act.wait(t)
                t = act.mark(act.h.activation(out=rs[:, tt:tt + 1], in_=ss[:, tt:tt + 1], func=AF.Ln,
                                              scale=1.0 / D, bias=dr["eps_sb"][:, 0:1]))
                act.wait(t)
                t = act.mark(act.h.activation(out=rstd[:, tt:tt + 1], in_=rs[:, tt:tt + 1], func=AF.Exp, scale=-0.5))
                act.wait(t)
                t = act.mark(act.h.activation(out=h2[:, tt, :], in_=xs[s][:, tt, :], func=AF.Copy,
                                              scale=rstd[:, tt:tt + 1]))
                tok_h2.append(t)
            return tok_h2

        def do_transposes(b, tok_h2):
            tok_h2T = []
            h2_reads = []
            for tt in range(2):
                bi, bank, fr = accT.get()
                bv = bank[:, :].bitcast(BF16)
                pe.wait(tok_h2[tt], fr)
                for c in range(8):
                    ins = pe.h.transpose(out=bv[:, c * 128:(c + 1) * 128], in_=h2[:, tt, c * 128:(c + 1) * 128],
                                         identity=dr["ident"][:, :])
                tmm = pe.mark(ins)
                h2_reads.append(tmm)
                dve.wait(tmm, st_["h2T_free"])
                t = dve.mark(dve.h.tensor_tensor(out=h2T[:, :, tt * 128:(tt + 1) * 128],
                                                 in0=bv.rearrange("p (c t) -> p c t", c=8),
                                                 in1=bcast_mid(g2, 128), op=ALU.mult))
                accT.release(bi, t)
                tok_h2T.append(t)
            st_["h2_free"] = h2_reads
            return tok_h2T

        st_ = {"h2_free": [], "h2T_free": [], "u2T_free": []}
        issue_loads(0)
        issue_mT(0)
        if NB > 1:
            issue_loads(1)
        tok_h2T = do_transposes(0, do_outproj_norm(0))
        for b in range(NB):
            s = b % 2
            t0 = b * TB
            tok_u = []
            up_reads = []
            for fc in range(32):
                bi, bank, fr = accU.get()
                pe.wait(tok_h2T, wtok_up, fr)
                for c in range(8):
                    ins = pe.h.matmul(bank[:, 0:TB], lhsT=w_up_sb[:, c, fc * 128:(fc + 1) * 128],
                                      rhs=h2T[:, c, :], start=(c == 0), stop=(c == 7))
                tmm = pe.mark(ins)
                up_reads.append(tmm)
                si, sqt, sfr = sqr.get()
                act.wait(tmm, sfr)
                ta = act.mark(act.h.activation(out=sqt[:, :], in_=bank[:, 0:TB], func=AF.Square))
                dve.wait(ta, st_["u2T_free"])
                t = dve.mark(dve.h.scalar_tensor_tensor(out=u2T[:, fc, :], in0=bank[:, 0:TB], scalar=0.0,
                                                        in1=sqt[:, :], op0=ALU.is_gt, op1=ALU.mult))
                accU.release(bi, t)
                sqr.release(si, t)
                tok_u.append(t)
            st_["h2T_free"] = [up_reads[-1]]
            nxt_h2 = do_outproj_norm(b + 1) if b + 1 < NB else None
            x2 = []
            dn_reads = []
            for tt in range(2):
                for hf in range(2):
                    bi, bank, fr = accA.get()
                    pe.wait(tok_u[-1], wtok_dn, fr)
                    for c in range(32):
                        ins = pe.h.matmul(bank[:, :], lhsT=u2T[:, c, tt * 128:(tt + 1) * 128],
                                          rhs=w_dn_sb[:, c, hf * 512:(hf + 1) * 512], start=(c == 0), stop=(c == 31))
                    tmm = pe.mark(ins)
                    dn_reads.append(tmm)
                    dve.wait(tmm)
                    t = dve.mark(dve.h.tensor_tensor(out=xs[s][:, tt, hf * 512:(hf + 1) * 512], in0=bank[:, :],
                                                     in1=xs[s][:, tt, hf * 512:(hf + 1) * 512], op=ALU.add))
                    accA.release(bi, t)
                    x2.append(t)
            st_["u2T_free"] = [dn_reads[-1]]
            if nxt_h2 is not None:
                tok_h2T = do_transposes(b + 1, nxt_h2)
            if not final:
                sp.wait(x2)
                t = sp.dma(xout[t0:t0 + TB, :].rearrange("(t p) d -> p t d", p=128), xs[s][:, :, :], st_sem[s])
                xs_free[s] = [t]
            else:
                fin = []
                for tt in range(2):
                    act.wait(x2)
                    t = act.mark(act.h.activation(out=fjunk[:, :], in_=xs[s][:, tt, :], func=AF.Square,
                                                  accum_out=ss[:, 2 + tt:3 + tt]))
                    act.wait(t)
                    t = act.mark(act.h.activation(out=rs[:, 2 + tt:3 + tt], in_=ss[:, 2 + tt:3 + tt], func=AF.Ln,
                                                  scale=1.0 / D, bias=dr["eps_sb"][:, 0:1]))
                    act.wait(t)
                    t = act.mark(act.h.activation(out=rstd[:, 2 + tt:3 + tt], in_=rs[:, 2 + tt:3 + tt], func=AF.Exp,
                                                  scale=-0.5))
                    dve.wait(t, tok_gF)
                    t = dve.mark(dve.h.scalar_tensor_tensor(out=xs[s][:, tt, :], in0=xs[s][:, tt, :],
                                                            scalar=rstd[:, 2 + tt:3 + tt], in1=gF[:, :],
                                                            op0=ALU.mult, op1=ALU.mult))
                    fin.append(t)
                sp.wait(fin)
                t = sp.dma(dr["y"][t0:t0 + TB, :].rearrange("(t p) d -> p t d", p=128), xs[s][:, :, :], st_sem[s])
                xs_free[s] = [t]
            if b + 2 < NB:
                issue_loads(b + 2)
        k.barrier()


def phase_A(k, l, dr):
    nc = k.nc
    pe, act, dve, pool, sp = k.pe, k.act, k.dve, k.pool, k.sp
    TB = 512
    NB = S // TB
    xin = dr["x_in"]
    with ExitStack() as es:
        def sb(name, shape, dt):
            return es.enter_context(nc.sbuf_tensor(f"{name}_L{l}", shape, dt))
        w_in_sb = sb("w_in_sb", [128, 8, INC], BF16)
        xs = [sb(f"xa{i}", [128, 4, 1024], F32) for i in range(2)]
        junk = sb("junkA", [128, 1024], BF16)
        hs = sb("hsA", [128, 4, 1024], BF16)
        hT = [sb(f"hTA{i}", [128, 8, TB], BF16) for i in range(2)]
        ss = sb("ssA", [128, 4], F32)
        rs = sb("rsA", [128, 4], F32)
        rstd = sb("rstdA", [128, 4], F32)
        NST = 8
        stg = [sb(f"stgA{i}", [128, 512], BF16) for i in range(NST)]
        stgf = [sb(f"stgAf{i}", [128, 512], F32) for i in range(2)]
        sig = [sb(f"sigA{i}", [128, 512], F32) for i in range(2)]

        wtok = []
        for dc in range(8):
            wtok.append(pool.dma(w_in_sb[:, dc, :], dr["w_in"][l, dc * 128:(dc + 1) * 128, :], k.sem(f"w_in{dc}")))

        xs_sem = [k.sem(f"A_xs{i}") for i in range(2)]
        xs_free = [[], []]
        ld_tok = {}

        def issue_loads(b):
            s = b % 2
            t0 = b * TB
            sp.wait(xs_free[s])
            ld_tok[b] = sp.dma(xs[s][:, :, :], xin[t0:t0 + TB, :].rearrange("(t p) d -> p t d", p=128), xs_sem[s])

        accT = Rot([k.ps[0], k.ps[1]])
        accM = Rot([k.ps[2], k.ps[3], k.ps[4], k.ps[5]])
        stg_r = Rot(stg)
        stg_sems = [k.sem(f"A_st{i}") for i in range(NST)]
        stgf_r = Rot(stgf)
        sig_r = Rot(sig)
        hT_free = [[], []]
        g1 = dr["g1_sb"][:, l * 8:(l + 1) * 8]
        lbt = sb("lbA", [128, 512], F32)
        omlt = sb("omlA", [128, 512], F32)
        sp.dma(lbt[:, :], dr["lbrep"][:, l, :], k.sem("consts"))
        tok_lb = sp.dma(omlt[:, :], dr["omlrep"][:, l, :], k.sem("consts"))
        lb, oml = lbt[:, :], omlt[:, :]

        def store_bf(eng_tok_fn, dst):
            si, st, fr = stg_r.get()
            t = eng_tok_fn(st, fr)
            sp.wait(t)
            td = sp.dma(dst, st[:, :], stg_sems[si])
            stg_r.release(si, td)

        st_ = {"hs_free": []}

        def norm_chain(b):
            s = b % 2
            tok_x = ld_tok[b]
            tok_hs = []
            for tt in range(4):
                act.wait(tok_x)
                t = act.mark(act.h.activation(out=junk[:, :], in_=xs[s][:, tt, :], func=AF.Square,
                                              accum_out=ss[:, tt:tt + 1]))
                act.wait(t)
                t = act.mark(act.h.activation(out=rs[:, tt:tt + 1], in_=ss[:, tt:tt + 1], func=AF.Ln,
                                              scale=1.0 / D, bias=dr["eps_sb"][:, 0:1]))
                act.wait(t)
                t = act.mark(act.h.activation(out=rstd[:, tt:tt + 1], in_=rs[:, tt:tt + 1], func=AF.Exp, scale=-0.5))
                act.wait(t, st_["hs_free"])
                t = act.mark(act.h.activation(out=hs[:, tt, :], in_=xs[s][:, tt, :], func=AF.Copy,
                                              scale=rstd[:, tt:tt + 1]))
                tok_hs.append(t)
            xs_free[s] = [tok_hs[-1]]
            return tok_hs

        def transposes(b, tok_hs):
            hTb = hT[b % 2]
            tok_hT = []
            hs_reads = []
            for tt in range(4):
                bi, bank, fr = accT.get()
                bv = bank[:, :].bitcast(BF16)
                pe.wait(tok_hs[tt], fr)
                for c in range(8):
                    ins = pe.h.transpose(out=bv[:, c * 128:(c + 1) * 128], in_=hs[:, tt, c * 128:(c + 1) * 128],
                                         identity=dr["ident"][:, :])
                tmm = pe.mark(ins)
                hs_reads.append(tmm)
                dve.wait(tmm, hT_free[b % 2])
                t = dve.mark(dve.h.tensor_tensor(out=hTb[:, :, tt * 128:(tt + 1) * 128],
                                                 in0=bv.rearrange("p (c t) -> p c t", c=8),
                                                 in1=bcast_mid(g1, 128), op=ALU.mult))
                accT.release(bi, t)
                tok_hT.append(t)
            st_["hs_free"] = hs_reads
            return tok_hT

        issue_loads(0)
        if NB > 1:
            issue_loads(1)
        tok_hT = transposes(0, norm_chain(0))
        for b in range(NB):
            s = b % 2
            t0 = b * TB
            hTc = hT[b % 2]
            nxt_hs = norm_chain(b + 1) if b + 1 < NB else None
            if b + 2 < NB:
                issue_loads(b + 2)
            last_mm = None
            fm = [("qT", 0, False), ("kT", 512, False), ("rqT", 1536, True), ("rgT", 3072, True)]
            for name, cbase, silu in fm:
                for h in range(4):
                    bi, bank, fr = accM.get()
                    pe.wait(tok_hT, wtok, fr)
                    c0 = cbase + h * 128
                    for c in range(8):
                        ins = pe.h.matmul(bank[:, :], lhsT=w_in_sb[:, c, c0:c0 + 128], rhs=hTc[:, c, :],
                                          start=(c == 0), stop=(c == 7))
                    tmm = pe.mark(ins)
                    last_mm = tmm

                    def fill(st, fr2, bank=bank, tmm=tmm, silu=silu, bi=bi):
                        if silu:
                            act.wait(tmm, fr2)
                            t = act.mark(act.h.activation(out=st[:, :], in_=bank[:, :], func=AF.Silu))
                        else:
                            dve.wait(tmm, fr2)
                            t = dve.mark(dve.h.tensor_copy(out=st[:, :], in_=bank[:, :]))
                        accM.release(bi, t)
                        return t
                    store_bf(fill, dr[name][h, :, t0:t0 + TB])
            nxt_hT = transposes(b + 1, nxt_hs) if nxt_hs is not None else None
            for tt in range(4):
                r0 = t0 + tt * 128
                for name, cbase in (("v", 1024), ("ri", 2560)):
                    bi, bank, fr = accM.get()
                    pe.wait(tok_hT, wtok, fr)
                    for c in range(8):
                        ins = pe.h.matmul(bank[:, :], lhsT=hTc[:, c, tt * 128:(tt + 1) * 128],
                                          rhs=w_in_sb[:, c, cbase:cbase + 512], start=(c == 0), stop=(c == 7))
                    tmm = pe.mark(ins)
                    last_mm = tmm

                    def fill(st, fr2, bank=bank, tmm=tmm, bi=bi):
                        dve.wait(tmm, fr2)
                        t = dve.mark(dve.h.tensor_copy(out=st[:, :], in_=bank[:, :]))
                        accM.release(bi, t)
                        return t
                    store_bf(fill, dr[name][r0:r0 + 128, :])
                bi, bank, fr = accM.get()
                pe.wait(tok_hT, wtok, fr)
                for c in range(8):
                    ins = pe.h.matmul(bank[:, :], lhsT=hTc[:, c, tt * 128:(tt + 1) * 128],
                                      rhs=w_in_sb[:, c, 2048:2560], start=(c == 0), stop=(c == 7))
                tmm = pe.mark(ins)
                last_mm = tmm
                gi, sg, gfr = sig_r.get()
                act.wait(tmm, gfr)
                t = act.mark(act.h.activation(out=sg[:, :], in_=bank[:, :], func=AF.Sigmoid))
                accM.release(bi, t)
                dve.wait(t, tok_lb)
                t = dve.mark(dve.h.tensor_tensor(out=sg[:, :], in0=sg[:, :], in1=oml, op=ALU.mult))
                dve.wait(t)
                tf = dve.mark(dve.h.tensor_tensor(out=sg[:, :], in0=sg[:, :], in1=lb, op=ALU.add))
                def fillk(st, fr2, sg=sg, tf=tf):
                    dve.wait(tf, fr2)
                    return dve.mark(dve.h.tensor_scalar(out=st[:, :], in0=sg[:, :], scalar1=-1.0, scalar2=1.0,
                                                        op0=ALU.mult, op1=ALU.add))
                si, st, sfr = stg_r.get()
                tk = fillk(st, sfr)
                sp.wait(tk)
                td = sp.dma(dr["kk"][r0:r0 + 128, :], st[:, :], stg_sems[si])
                stg_r.release(si, td)
                fi, sf, ffr = stgf_r.get()
                act.wait(tf, ffr)
                tg = act.mark(act.h.activation(out=sf[:, :], in_=sg[:, :], func=AF.Ln))
                sig_r.release(gi, tg, tk)
                si, st, sfr = stg_r.get()
                dve.wait(tg, sfr)
                thi = dve.mark(dve.h.tensor_copy(out=st[:, :], in_=sf[:, :]))
                si2, st2, sfr2 = stg_r.get()
                dve.wait(thi, sfr2)
                tlo = dve.mark(dve.h.tensor_tensor(out=st2[:, :], in0=sf[:, :], in1=st[:, :], op=ALU.subtract))
                stgf_r.release(fi, tlo)
                sp.wait(tlo)
                td = sp.dma(dr["ghi"][r0:r0 + 128, :], st[:, :], stg_sems[si])
                stg_r.release(si, td)
                td = sp.dma(dr["glo"][r0:r0 + 128, :], st2[:, :], stg_sems[si2])
                stg_r.release(si2, td)
            hT_free[b % 2] = [last_mm]
            tok_hT = nxt_hT
        k.barrier()


def phase_B(k, l, dr):
    nc = k.nc
    pe, act, dve, pool, sp = k.pe, k.act, k.dve, k.pool, k.sp
    QC = 512
    NQ = S // QC
    qT, kT, v, mixT = dr["qT"], dr["kT"], dr["v"], dr["mixT"]
    with ExitStack() as es:
        def sb(name, shape, dt):
            return es.enter_context(nc.sbuf_tensor(f"{name}_L{l}", shape, dt))
        qp1 = [sb(f"qp1_{i}", [128, S], BF16) for i in range(2)]
        qp2 = [sb(f"qp2_{i}", [128, S], BF16) for i in range(2)]
        kTs = [sb(f"kTs{i}", [128, S], BF16) for i in range(2)]
        Vs = [sb(f"Vs{i}", [128, 32, 128], BF16) for i in range(2)]
        P = [sb(f"P{i}", [128, 2, QC], BF16) for i in range(2)]
        r1 = sb("Br1", [128, QC], F32)
        r2 = sb("Br2", [128, QC], F32)
        ta = sb("Bta", [128, QC], F32)
        tb = sb("Btb", [128, QC], F32)
        to = sb("Bto", [128, QC], F32)
        tsq = sb("Btsq", [128, QC], BF16)
        trs = sb("Btrs", [128, QC], F32)
        ystg = [sb(f"Bys{i}", [128, QC], BF16) for i in range(2)]

        zt = []
        for i in range(2):
            zt.append(pool.mark(pool.h.memset(qp1[i][64:128, :], 0.0)))
            zt.append(pool.mark(pool.h.memset(qp2[i][0:64, :], 0.0)))

        hd_sem = [k.sem(f"B_hd{i}") for i in range(2)]
        hd_free = [[], []]
        hd_tok = {}

        def issue_head_loads(h):
            s = h % 2
            sp.wait(hd_free[s])
            sp.dma(kTs[s][:, :], kT[h, :, :], hd_sem[s])
            sp.dma(qp1[s][0:64, :], qT[h, 0:64, :], hd_sem[s])
            sp.dma(qp2[s][64:128, :], qT[h, 64:128, :], hd_sem[s])
            hd_tok[h] = sp.dma(Vs[s][:, :, :], v.rearrange("(t p) c -> p t c", p=128)[:, :, h * 128:(h + 1) * 128],
                               hd_sem[s])

        accS = Rot([(k.ps[4], k.ps[5]), (k.ps[6], k.ps[7])])
        S2b = [k.psum[:, (4 + 2 * i) * 512:(6 + 2 * i) * 512].rearrange("p (m c) -> p m c", m=2) for i in range(2)]
        O1, O2, s1, s2 = k.ps[0], k.ps[1], k.ps[2], k.ps[3]
        Prot = Rot(P)
        ysr = Rot(ystg)
        ys_sems = [k.sem(f"B_ys{i}") for i in range(2)]
        ident = dr["ident"]
        ones_bf = dr["ones_bf"]
        ones_f = dr["ones_f"]
        state = {"O_free": [], "fin_free": []}
        deferred = []

        def emit_S(h, qc, kt):
            s = h % 2
            q0 = qc * QC
            c0 = max(0, kt * 128 - q0)
            bi, (S1, S2), fr = accS.get()
            pe.wait(hd_tok[h], zt, fr)
            near = []
            if kt * 128 >= q0:
                near.append((c0, 0))
            if q0 <= kt * 128 + 128 < q0 + QC:
                near.append((kt * 128 + 128 - q0, 128))
            for Sb, qp in ((S1, qp1[s]), (S2, qp2[s])):
                ins = pe.h.matmul(Sb[:, c0:QC], lhsT=kTs[s][:, kt * 128:(kt + 1) * 128], rhs=qp[:, q0 + c0:q0 + QC],
                                  start=True, stop=(len(near) == 0))
                for i, (cs, go) in enumerate(near):
                    ins = pe.h.matmul(Sb[:, cs:cs + 128], lhsT=ident[:, :], rhs=dr["G_sb"][:, h, go:go + 128],
                                      start=False, stop=(i == len(near) - 1))
            return bi, S1, S2, pe.mark(ins), c0

        def emit_exp(h, sinfo):
            bi, S1, S2, tmm, c0 = sinfo
            pi, Pt, pfr = Prot.get()
            act.wait(tmm, pfr)
            t2 = act.mark(act.h.activation(out=Pt[:, :, c0:QC], in_=S2b[bi][:, :, c0:QC], func=AF.Exp, scale=0.125,
                                           bias=dr["cb_sb"][:, h:h + 1]))
            accS.release(bi, t2)
            return pi, Pt[:, 0, :], Pt[:, 1, :], t2, c0

        def emit_PV(h, kt, nkt, pinfo):
            s = h % 2
            pi, P1, P2, texp, c0 = pinfo
            pe.wait(texp)
            if kt == 0:
                pe.wait(state["O_free"])
            first, last = (kt == 0), (kt == nkt - 1)
            for Ob, sbk, Pm in ((O1, s1, P1), (O2, s2, P2)):
                pe.h.matmul(Ob[:, c0:QC], lhsT=Vs[s][:, kt, :], rhs=Pm[:, c0:QC], start=first, stop=last)
                ins = pe.h.matmul(sbk[:, c0:QC], lhsT=ones_bf[:, :], rhs=Pm[:, c0:QC], start=first, stop=last)
            t = pe.mark(ins)
            Prot.release(pi, t)
            return t

        def emit_finalize(h, qc, tlast):
            q0 = qc * QC
            dve.wait(tlast, state["fin_free"])
            dve.h.tensor_copy(out=r1[:, :], in_=s1[:, :])
            dve.h.tensor_copy(out=r2[:, :], in_=s2[:, :])
            dve.h.tensor_copy(out=ta[:, :], in_=O1[:, :])
            t = dve.mark(dve.h.tensor_copy(out=tb[:, :], in_=O2[:, :]))
            state["O_free"] = [t]
            dve.wait(t)
            t = dve.mark(dve.h.reciprocal(out=r1[:, :], in_=r1[:, :]))
            t = dve.mark(dve.h.reciprocal(out=r2[:, :], in_=r2[:, :]))
            dve.wait(t)
            t = dve.mark(dve.h.tensor_tensor(out=ta[:, :], in0=ta[:, :], in1=r1[:, :], op=ALU.mult))
            t = dve.mark(dve.h.tensor_tensor(out=tb[:, :], in0=tb[:, :], in1=r2[:, :], op=ALU.mult))
            dve.wait(t)
            to_tok = dve.mark(dve.h.scalar_tensor_tensor(out=to[:, :], in0=tb[:, :], scalar=dr["neglam_sb"][:, l:l + 1],
                                                         in1=ta[:, :], op0=ALU.mult, op1=ALU.add))

            sq_tok = {}

            def part1b():
                act.wait(to_tok)
                sq_tok["t"] = act.mark(act.h.activation(out=tsq[:, :], in_=to[:, :], func=AF.Square))

            def part2():
                tsq_tok = sq_tok["t"]
                bi = accS.i % len(accS.items)
                (B1, B2), fr = accS.items[bi], accS.free[bi]
                pe.wait(tsq_tok, fr)
                tm = pe.mark(pe.h.matmul(B1[:, :], lhsT=ones_bf[:, :], rhs=tsq[:, :], start=True, stop=True))
                act.wait(tm)
                t = act.mark(act.h.activation(out=trs[:, :], in_=B1[:, :], func=AF.Ln, scale=1.0 / 128,
                                              bias=dr["eps_sb"][:, 0:1]))
                accS.release(bi, t)
                act.wait(t)
                t = act.mark(act.h.activation(out=trs[:, :], in_=trs[:, :], func=AF.Exp, scale=-0.5))
                yi, ys, yfr = ysr.get()
                dve.wait(t, yfr)
                t = dve.mark(dve.h.scalar_tensor_tensor(out=ys[:, :], in0=to[:, :], scalar=dr["gAs_sb"][:, l:l + 1],
                                                        in1=trs[:, :], op0=ALU.mult, op1=ALU.mult))
                state["fin_free"] = [t]
                sp.wait(t)
                td = sp.dma(mixT[h * 128:(h + 1) * 128, q0:q0 + QC], ys[:, :], ys_sems[yi])
                ysr.release(yi, td)
            deferred.append([8, part1b])
            deferred.append([10, part2])

        def tick_deferred(force=False):
            for d in list(deferred):
                d[0] -= 1
                if d[0] <= 0 or force:
                    d[1]()
                    deferred.remove(d)

        issue_head_loads(0)
        for h in range(4):
            if h + 1 < 4:
                issue_head_loads(h + 1)
            pairs = [(qc, kt) for qc in reversed(range(NQ)) for kt in range(4 * qc + 4)]
            sinfo = emit_S(h, *pairs[0])
            for j, (qc, kt) in enumerate(pairs):
                nxt = emit_S(h, *pairs[j + 1]) if j + 1 < len(pairs) else None
                pinfo = emit_exp(h, sinfo)
                nkt = 4 * qc + 4
                tpv = emit_PV(h, kt, nkt, pinfo)
                tick_deferred()
                if kt == nkt - 1:
                    tick_deferred(force=True)
                    emit_finalize(h, qc, tpv)
                sinfo = nxt
            tick_deferred(force=True)
            hd_free[h % 2] = [tpv]
        k.barrier()


def phase_C(k, l, dr, extra=()):
    nc = k.nc
    pe, act, dve, pool, sp = k.pe, k.act, k.dve, k.pool, k.sp
    NT = S // 128
    mixT = dr["mixT"]
    with ExitStack() as es:
        def sb(name, shape, dt):
            return es.enter_context(nc.sbuf_tensor(f"{name}_L{l}", shape, dt))
        gt = [sb(f"Cg{i}", [128, 2, 512], BF16) for i in range(2)]
        kkt = [sb(f"Ckk{i}", [128, 512], BF16) for i in range(2)]
        Vt = [sb(f"CV{i}", [128, 512], BF16) for i in range(2)]
        rqt = [sb(f"Crq{i}", [128, 4, 128], BF16) for i in range(2)]
        rgt = [sb(f"Crg{i}", [128, 4, 128], BF16) for i in range(2)]
        EA = sb("CEA", [128, 512], F32)
        EnA = sb("CEnA", [128, 512], F32)
        EH = sb("CEH", [128, 512], F32)
        ee = sb("Cee", [128, 4, 4], F32)
        qtl = sb("Cqtl", [128, 4, 128], BF16)
        ktl = sb("Cktl", [128, 4, 128], BF16)
        khb = sb("Ckhb", [128, 2, 512], BF16)
        attb = sb("Cattb", [128, 2, 4, 64], BF16)
        Smid = sb("CSmid", [128, 4, 128], BF16)
        stt = sb("Cstate", [128, 4, 128], F32)
        tmp = sb("Ctmp", [128, 4, 128], F32)
        sq = sb("Csq", [128, 512], BF16)
        rsn = sb("Crsn", [128, 512], F32)
        y1 = sb("Cy1", [128, 2, 4, 64], F32)
        ystg = [sb(f"Cys{i}", [128, 4, 128], BF16) for i in range(2)]

        init = [pool.mark(pool.h.memset(khb[:, :, :], 0.0)),
                pool.mark(pool.h.memset(attb[:, :, :, :], 0.0)),
                pool.mark(pool.h.memset(stt[:, :, :], 0.0))]
        AT, EB, MB, KB, ATT, OB, UB, NB = k.ps
        KBv = KB[:, :].bitcast(BF16)

        ld_sem = [k.sem(f"C_ld{i}") for i in range(2)]
        ld_free = [[], []]
        ld_tok = {}

        def issue_loads(tt):
            s = tt % 2
            t0 = tt * 128
            sp.wait(ld_free[s])
            sp.dma(gt[s][:, 0, :], dr["ghi"][t0:t0 + 128, :], ld_sem[s])
            sp.dma(gt[s][:, 1, :], dr["glo"][t0:t0 + 128, :], ld_sem[s])
            sp.dma(kkt[s][:, :], dr["kk"][t0:t0 + 128, :], ld_sem[s])
            sp.dma(Vt[s][:, :], dr["ri"][t0:t0 + 128, :], ld_sem[s])
            sp.dma(rqt[s][:, :, :], dr["rqT"][:, :, t0:t0 + 128].rearrange("h p t -> p h t"), ld_sem[s])
            ld_tok[tt] = sp.dma(rgt[s][:, :, :], dr["rgT"][:, :, t0:t0 + 128].rearrange("h p t -> p h t"), ld_sem[s])

        ys_sems = [k.sem(f"C_ys{i}") for i in range(2)]
        ysr = Rot(ystg)
        rd = {}

        def R(name):
            return rd.get(name, [])

        M1, Em, M2, tri = dr["M1_sb"], dr["Em_sb"], dr["M2_sb"], dr["tri_sb"]
        tok_state = init[2]
        issue_loads(0)
        for tt in range(NT):
            s = tt % 2
            t0 = tt * 128
            if tt + 1 < NT:
                issue_loads(tt + 1)
            if tt < len(extra):
                extra[tt]()
            tl = ld_tok[tt]
            pe.wait(tl, R("AT"))
            for h in range(4):
                for j in range(2):
                    ins = pe.h.matmul(AT[:, h * 128:(h + 1) * 128], lhsT=gt[s][:, j, h * 128:(h + 1) * 128], rhs=M1[:, :],
                                      start=(j == 0), stop=(j == 1))
            t_AT = pe.mark(ins)
            pe.wait(R("EB"))
            for h in range(4):
                for j in range(2):
                    ins = pe.h.matmul(EB[:, h * 4:(h + 1) * 4], lhsT=gt[s][:, j, h * 128:(h + 1) * 128], rhs=Em[:, :],
                                      start=(j == 0), stop=(j == 1))
            t_EB = pe.mark(ins)
            pe.wait(R("MB"))
            pe.h.matmul(MB[:, :], lhsT=M2[:, :], rhs=gt[s][:, 0, :], start=True, stop=False)
            t_MB = pe.mark(pe.h.matmul(MB[:, :], lhsT=M2[:, :], rhs=gt[s][:, 1, :], start=False, stop=True))
            pe.wait(R("KB"))
            for h in range(4):
                ins = pe.h.transpose(out=KBv[:, h * 128:(h + 1) * 128], in_=kkt[s][:, h * 128:(h + 1) * 128],
                                     identity=dr["ident"][:, :])
            t_KB = pe.mark(ins)
            act.wait(t_AT, R("EA"))
            t_EA = act.mark(act.h.activation(out=EA[:, :], in_=AT[:, :], func=AF.Exp))
            act.wait(R("EnA"))
            t_EnA = act.mark(act.h.activation(out=EnA[:, :], in_=AT[:, :], func=AF.Exp, scale=-1.0))
            rd["AT"] = [t_EnA]
            act.wait(t_MB, R("EH"))
            t_EH = act.mark(act.h.activation(out=EH[:, :], in_=MB[:, :], func=AF.Exp))
            rd["MB"] = [t_EH]
            act.wait(t_EB, R("ee"))
            t_ee = act.mark(act.h.activation(out=ee[:, :, :], in_=EB[:, 0:16].rearrange("p (h j) -> p h j", h=4),
                                             func=AF.Exp))
            rd["EB"] = [t_ee]
            pool.wait(t_EA, tl, R("qtl"))
            t_qtl = pool.mark(pool.h.tensor_tensor(out=qtl[:, :, :], in0=rqt[s][:, :, :],
                                                   in1=EA[:, :].rearrange("p (h t) -> p h t", h=4), op=ALU.mult))
            rd["EA"] = [t_qtl]
            dve.wait(t_KB, t_EnA, R("ktl"))
            t_ktl = dve.mark(dve.h.tensor_tensor(out=ktl[:, :, :], in0=KBv[:, 0:512].rearrange("p (h t) -> p h t", h=4),
                                                 in1=EnA[:, :].rearrange("p (h t) -> p h t", h=4), op=ALU.mult))
            rd["KB"] = [t_ktl]
            rd["EnA"] = [t_ktl]
            pool.wait(t_EH, R("khb"), init[0])
            for c in range(2):
                t_khb = pool.mark(pool.h.tensor_tensor(out=khb[c * 64:(c + 1) * 64, c, :], in0=kkt[s][c * 64:(c + 1) * 64, :],
                                                       in1=EH[c * 64:(c + 1) * 64, :], op=ALU.mult))
            rd["EH"] = [t_khb]
            pe.wait(t_qtl, t_ktl, R("ATT"))
            for c in range(2):
                for h in range(4):
                    ins = pe.h.matmul(ATT[:, c * 256 + h * 64:c * 256 + (h + 1) * 64], lhsT=ktl[:, h, :],
                                      rhs=qtl[:, h, c * 64:(c + 1) * 64], start=True, stop=True)
            t_ATT = pe.mark(ins)
            rd["ktl"] = [t_ATT]
            dve.wait(t_ATT, R("attb"), init[1])
            for c in range(2):
                t_attb = dve.mark(dve.h.tensor_tensor(
                    out=attb[c * 64:(c + 1) * 64, c, :, :],
                    in0=ATT[c * 64:(c + 1) * 64, c * 256:(c + 1) * 256].rearrange("p (h t) -> p h t", h=4),
                    in1=tri[c * 64:(c + 1) * 64, :, :], op=ALU.mult))
            rd["ATT"] = [t_attb]
            eev = ee[:, :, :]
            for c in range(2):
                dve.wait(tok_state, t_ee, R("Smid"))
                t_Smid = dve.mark(dve.h.tensor_tensor(out=Smid[:, :, :], in0=stt[:, :, :],
                                                      in1=bcast_mid(eev[:, :, c], 128), op=ALU.mult))
                pe.wait(t_attb, t_Smid, tl, R("OB") if c == 0 else [])
                for h in range(4):
                    o_ap = OB[:, c * 256 + h * 64:c * 256 + (h + 1) * 64]
                    pe.h.matmul(o_ap, lhsT=Vt[s][:, h * 128:(h + 1) * 128], rhs=attb[:, c, h, :], start=True, stop=False)
                    ins = pe.h.matmul(o_ap, lhsT=Smid[:, h, :], rhs=qtl[:, h, c * 64:(c + 1) * 64], start=False, stop=True)
                t_OB = pe.mark(ins)
                rd["Smid"] = [t_OB]
                pe.wait(t_khb, R("UB"))
                for h in range(4):
                    ins = pe.h.matmul(UB[:, h * 128:(h + 1) * 128], lhsT=khb[:, c, h * 128:(h + 1) * 128],
                                      rhs=Vt[s][:, h * 128:(h + 1) * 128], start=True, stop=True)
                t_UB = pe.mark(ins)
                pool.wait(tok_state, t_ee, t_Smid, R("tmp"))
                t_tmp = pool.mark(pool.h.tensor_tensor(out=tmp[:, :, :], in0=stt[:, :, :],
                                                       in1=bcast_mid(eev[:, :, 2 + c], 128), op=ALU.mult))
                dve.wait(t_tmp, t_UB, t_Smid)
                tok_state = dve.mark(dve.h.tensor_tensor(out=stt[:, :, :], in0=tmp[:, :, :],
                                                         in1=UB[:, :].rearrange("p (h v) -> p h v", h=4), op=ALU.add))
                rd["UB"] = [tok_state]
                rd["tmp"] = [tok_state]
            rd["attb"] = [t_OB]
            rd["qtl"] = [t_OB]
            rd["khb"] = [t_UB]
            rd["ee"] = [tok_state]
            act.wait(t_OB, R("sq"))
            t_sq = act.mark(act.h.activation(out=sq[:, :], in_=OB[:, :], func=AF.Square))
            pe.wait(t_sq, R("NB"))
            t_NB = pe.mark(pe.h.matmul(NB[:, :], lhsT=dr["ones_bf"][:, :], rhs=sq[:, :], start=True, stop=True))
            rd["sq"] = [t_NB]
            act.wait(t_NB, R("rsn"))
            t = act.mark(act.h.activation(out=rsn[:, :], in_=NB[:, :], func=AF.Ln, scale=1.0 / 128,
                                          bias=dr["eps_sb"][:, 0:1]))
            rd["NB"] = [t]
            act.wait(t)
            t = act.mark(act.h.activation(out=rsn[:, :], in_=rsn[:, :], func=AF.Exp, scale=-0.5))
            dve.wait(t, R("y1"))
            t_y1 = dve.mark(dve.h.scalar_tensor_tensor(out=y1[:, :, :, :],
                                                       in0=OB[:, :].rearrange("p (c h t) -> p c h t", c=2, h=4),
                                                       scalar=dr["gH_sb"][:, l:l + 1],
                                                       in1=rsn[:, :].rearrange("p (c h t) -> p c h t", c=2, h=4),
                                                       op0=ALU.mult, op1=ALU.mult))
            rd["OB"] = [t_y1]
            rd["rsn"] = [t_y1]
            yi, ys, yfr = ysr.get()
            pool.wait(t_y1, yfr)
            t_ys = pool.mark(pool.h.tensor_tensor(out=ys[:, :, :].rearrange("p h (c t) -> p c h t", c=2),
                                                  in0=y1[:, :, :, :],
                                                  in1=rgt[s][:, :, :].rearrange("p h (c t) -> p c h t", c=2), op=ALU.mult))
            rd["y1"] = [t_ys]
            sp.wait(t_ys)
            td = sp.dma(mixT[512:1024, t0:t0 + 128].rearrange("(h p) t -> p h t", p=128), ys[:, :, :], ys_sems[yi])
            ysr.release(yi, td)
            ld_free[s] = [t_ys, t_UB, t_OB, t_KB, t_MB]
        k.barrier()


def t5_bucket_np(rel):
    n_half, max_exact = 16, 8
    ret = np.where(rel > 0, n_half, 0)
    n = np.abs(rel)
    nf = np.maximum(n, 1).astype(np.float32)
    large = max_exact + (np.log(nf / np.float32(max_exact)) / np.float32(math.log(128 / max_exact))
                         * np.float32(n_half - max_exact)).astype(np.int32)
    large = np.minimum(large, n_half - 1)
    return ret + np.where(n < max_exact, n, large)


LAM_INIT = [0.8 - 0.6 * math.exp(-0.3 * l) for l in range(DEPTH)]

CF = {}
_o = 0
for _n, _w in (("ones_f", 128), ("M1", 128), ("M2", 128), ("Em", 4), ("eps", 1), ("tri", 256), ("OH8", 384),
               ("maskadd", 128), ("oml_init", 4), ("neg_lam_init", 4)):
    CF[_n] = (_o, _w)
    _o += _w
CF_W = _o
CB = {"ident": (0, 128), "J": (128, 128), "ones_bf": (256, 128), "M1b": (384, 128), "M2b": (512, 128), "Emb": (640, 4)}
CB_W = 644


def host_consts():
    cf = np.zeros((128, CF_W), np.float32)

    def put(name, arr):
        o, w = CF[name]
        cf[:arr.shape[0], o:o + w] = arr.reshape(arr.shape[0], -1)

    put("ones_f", np.ones((128, 128), np.float32))
    tp = np.arange(128)[:, None]
    t = np.arange(128)[None, :]
    same = (tp // 64) == (t // 64)
    mid = (t // 64) * 64 + 31
    put("M1", (same & (tp <= t)).astype(np.float32) - (same & (tp <= mid)).astype(np.float32))
    put("M2", (same & (tp > t)).astype(np.float32))
    Em = np.zeros((128, 4), np.float32)
    ar = np.arange(128)
    for c in range(2):
        Em[:, c] = ((ar // 64 == c) & (ar <= c * 64 + 31)).astype(np.float32)
        Em[:, 2 + c] = (ar // 64 == c).astype(np.float32)
    put("Em", Em)
    put("eps", np.full((128, 1), EPS, np.float32))
    s_ = (ar % 64)[:, None, None]
    tq = np.arange(64)[None, None, :]
    put("tri", np.broadcast_to((s_ <= tq), (128, 4, 64)).astype(np.float32))
    i = np.arange(384)
    rel = 127 - i
    bk = t5_bucket_np(rel)
    oh = np.zeros((32, 384), np.float32)
    oh[bk, i] += 8.0
    oh[15, :] -= 8.0
    oh[:, 383] = 0.0
    put("OH8", oh)
    p = np.arange(128)[:, None]
    j = np.arange(128)[None, :]
    put("maskadd", np.where((p // 64) > (j // 64), -240000.0, 0.0).astype(np.float32))
    put("oml_init", np.broadcast_to(np.array([1.0 - x for x in LAM_INIT], np.float32)[None, :], (128, 4)))
    put("neg_lam_init", np.broadcast_to(np.array([-x for x in LAM_INIT], np.float32)[None, :], (128, 4)))
    cb = np.zeros((128, CB_W), np.float32)
    cb[:, 0:128] = np.eye(128)
    cb[:, 128:256] = np.eye(128)[::-1]
    cb[:, 256:384] = 1.0
    for nm in ("M1", "M2", "Em"):
        o, w = CF[nm]
        ob, wb = CB[nm + "b"]
        cb[:, ob:ob + wb] = cf[:, o:o + w]
    return cf, cb.astype(ml_dtypes.bfloat16)


def setup(k, dr, ext):
    nc = k.nc
    pe, act, dve, pool, sp = k.pe, k.act, k.dve, k.pool, k.sp
    cs = k.sem("consts")
    cf = nc.alloc_sbuf_tensor("cf_sb", [128, CF_W], F32)
    cbt = nc.alloc_sbuf_tensor("cb_sb16", [128, CB_W], BF16)
    toks = [sp.dma(cf[:, :], ext["cf"], cs), sp.dma(cbt[:, :], ext["cb16"], cs)]

    def fv(name):
        o, w = CF[name]
        return cf[:, o:o + w]
    dr["ones_f"] = fv("ones_f")
    dr["M1_sb"] = cbt[:, 384:512]
    dr["M2_sb"] = cbt[:, 512:640]
    dr["Em_sb"] = cbt[:, 640:644]
    dr["eps_sb"] = fv("eps")
    dr["tri_sb"] = fv("tri").rearrange("p (h t) -> p h t", h=4)
    dr["ident"] = cbt[:, 0:128]
    Jm = cbt[:, 128:256]
    dr["ones_bf"] = cbt[:, 256:384]
    for name, w in (("g1", 32), ("g2", 32), ("gA", 4), ("gH", 4)):
        t = nc.alloc_sbuf_tensor(name + "_sb", [128, w], F32)
        toks.append(sp.dma(t[:, :], ext[name], cs))
        dr[name + "_sb"] = t
    dr["final_g"] = ext["final_g"]
    neglam = nc.alloc_sbuf_tensor("neglam_sb", [128, 4], F32)
    gAs = nc.alloc_sbuf_tensor("gAs_sb", [128, 4], F32)
    cbias = nc.alloc_sbuf_tensor("cbias_sb", [128, 4], F32)
    G = nc.alloc_sbuf_tensor("G_sb", [128, 4, 256], BF16)
    dr["neglam_sb"], dr["gAs_sb"], dr["cb_sb"], dr["G_sb"] = neglam, gAs, cbias, G
    toks.append(sp.dma(cbias[:, :], ext["rel_bias"][15:16, :].partition_broadcast(128), cs))
    with ExitStack() as es:
        def sb(name, shape, dt):
            return es.enter_context(nc.sbuf_tensor(name, shape, dt))
        lg = sb("S_lg", [128, 4, 512], F32)
        lb = sb("S_lb", [128, 4, 512], F32)
        oml = sb("S_oml", [128, 4, 512], F32)
        ssum = sb("S_sum", [128, 512], F32)
        lq = sb("S_lq", [128, 4, 4, 64], F32)
        pr = sb("S_pr", [128, 4, 2, 64], F32)
        dd = sb("S_dd", [128, 4, 2], F32)
        rb = sb("S_rb", [32, 4], F32)
        tv = sb("S_tv", [4, 384], F32)
        tvb = sb("S_tvb", [4, 384], BF16)
        Hk = sb("S_H", [128, 4, 256], BF16)
        toks.append(sp.dma(lg[:, :, :], ext["lb_logits"].partition_broadcast(128), cs))
        toks.append(sp.dma(lq[:, :, :, :], ext["lam_qk"].partition_broadcast(128), cs))
        toks.append(sp.dma(rb[:, :], ext["rel_bias"], cs))
        for e in k.engs:
            e.wait(toks[-1])
        t = act.mark(act.h.activation(out=lg[:, :, :], in_=lg[:, :, :], func=AF.Exp))
        dve.wait(t)
        t = dve.mark(dve.h.tensor_tensor(out=ssum[:, :], in0=lg[:, 0, :], in1=lg[:, 1, :], op=ALU.add))
        dve.wait(t)
        t = dve.mark(dve.h.tensor_tensor(out=ssum[:, :], in0=ssum[:, :], in1=lg[:, 2, :], op=ALU.add))
        dve.wait(t)
        t = dve.mark(dve.h.tensor_tensor(out=ssum[:, :], in0=ssum[:, :], in1=lg[:, 3, :], op=ALU.add))
        dve.wait(t)
        t = dve.mark(dve.h.reciprocal(out=ssum[:, :], in_=ssum[:, :]))
        dve.wait(t)
        t = dve.mark(dve.h.memset(lb[:, 0, :], 0.0))
        t = dve.mark(dve.h.tensor_tensor(out=lb[:, 1, :], in0=lg[:, 1, :], in1=ssum[:, :], op=ALU.mult))
        t = dve.mark(dve.h.tensor_tensor(out=lb[:, 2, :], in0=lg[:, 2, :], in1=ssum[:, :], op=ALU.mult))
        t = dve.mark(dve.h.tensor_tensor(out=lb[:, 3, :], in0=lg[:, 3, :], in1=ssum[:, :], op=ALU.mult))
        dve.wait(t)
        t = dve.mark(dve.h.tensor_tensor(out=lb[:, 2, :], in0=lb[:, 2, :], in1=lb[:, 1, :], op=ALU.add))
        dve.wait(t)
        t = dve.mark(dve.h.tensor_tensor(out=lb[:, 3, :], in0=lb[:, 3, :], in1=lb[:, 2, :], op=ALU.add))
        dve.wait(t)
        t = dve.mark(dve.h.tensor_scalar(out=oml[:, :, :], in0=lb[:, :, :], scalar1=-1.0, scalar2=1.0,
                                         op0=ALU.mult, op1=ALU.add))
        sp.wait(t)
        sp.dma(dr["lbrep"], lb[:, :, :], cs)
        sp.dma(dr["omlrep"], oml[:, :, :], cs)
        t = dve.mark(dve.h.tensor_tensor(out=pr[:, :, 0, :], in0=lq[:, :, 0, :], in1=lq[:, :, 1, :], op=ALU.mult))
        t = dve.mark(dve.h.tensor_tensor(out=pr[:, :, 1, :], in0=lq[:, :, 2, :], in1=lq[:, :, 3, :], op=ALU.mult))
        dve.wait(t)
        t = dve.mark(dve.h.reduce_sum(out=dd[:, :, :], in_=pr[:, :, :, :], axis=AX.X))
        act.wait(t)
        t = act.mark(act.h.activation(out=dd[:, :, :], in_=dd[:, :, :], func=AF.Exp))
        dve.wait(t)
        t = dve.mark(dve.h.tensor_tensor(out=neglam[:, :], in0=dd[:, :, 1], in1=dd[:, :, 0], op=ALU.subtract))
        dve.wait(t)
        t = dve.mark(dve.h.tensor_tensor(out=neglam[:, :], in0=neglam[:, :], in1=fv("neg_lam_init"), op=ALU.add))
        t = dve.mark(dve.h.tensor_tensor(out=gAs[:, :], in0=dr["gA_sb"][:, :], in1=fv("oml_init"), op=ALU.mult))
        o8, w8 = CF["OH8"]
        tm = pe.mark(pe.h.matmul(k.ps[0][0:4, 0:384], lhsT=rb[:, :], rhs=cf[0:32, o8:o8 + w8], start=True, stop=True))
        dve.wait(tm)
        t = dve.mark(dve.h.tensor_copy(out=tvb[:, :], in_=k.ps[0][0:4, 0:384]))
        sp.wait(t)
        scr = dr["tvec"]
        t = sp.dma(scr, tvb[:, :], cs)
        sp.wait(t)
        hank = bass.AP(tensor=scr.tensor, offset=scr.offset, ap=[[1, 128], [384, 4], [1, 256]])
        t = sp.dma(Hk[:, :, :], hank, cs)
        pe.wait(t)
        pe.h.matmul(k.ps[1][:, :], lhsT=Jm, rhs=Hk[:, 0:2, :], start=True, stop=True)
        tm = pe.mark(pe.h.matmul(k.ps[2][:, :], lhsT=Jm, rhs=Hk[:, 2:4, :], start=True, stop=True))
        dve.wait(tm)
        for hh, bank in ((0, k.ps[1]), (1, k.ps[2])):
            bv = bank[:, :].rearrange("p (h j) -> p h j", h=2)
            t = dve.mark(dve.h.tensor_tensor(out=G[:, 2 * hh:2 * hh + 2, 0:128], in0=bv[:, :, 0:128],
                                             in1=fv("maskadd").unsqueeze(1).broadcast_to([128, 2, 128]), op=ALU.add))
            t = dve.mark(dve.h.tensor_copy(out=G[:, 2 * hh:2 * hh + 2, 128:256], in_=bv[:, :, 128:256]))
        k.barrier()


def build_full(nlayers=DEPTH, debug=False):
    nc = bass.Bass("TRN2", target_bir_lowering=False)
    k = K(nc)

    def din(name, shape, dt=F32):
        return nc.dram_tensor(name, shape, dt, kind="ExternalInput").ap()

    def dscr(name, shape, dt):
        return nc.dram_tensor(name, shape, dt, kind="Internal").ap()

    ext = {
        "x": din("x", [S, D]),
        "w_in": din("w_in", [DEPTH, D, INC]),
        "w_out": din("w_out", [DEPTH, D, D]),
        "w_up": din("w_up", [DEPTH, D, DFF]),
        "w_down": din("w_down", [DEPTH, DFF, D]),
        "lb_logits": din("lb_logits", [DEPTH, 512]),
        "lam_qk": din("lam_qk", [DEPTH, 4, 64]),
        "rel_bias": din("rel_bias", [32, 4]),
        "final_g": din("final_g", [D]),
        "g1": din("g1", [128, 32]),
        "g2": din("g2", [128, 32]),
        "gA": din("gA", [128, 4]),
        "gH": din("gH", [128, 4]),
        "cf": din("cf", [128, CF_W]),
        "cb16": din("cb16", [128, CB_W], BF16),
    }
    y = nc.dram_tensor("y", [S, D], F32, kind="ExternalOutput").ap()
    dr = {"y": y}
    for n in ("w_in", "w_out", "w_up", "w_down"):
        dr[n] = ext[n]
    xres = dscr("xres", [S, D], F32)
    for n in ("qT", "kT", "rqT", "rgT"):
        dr[n] = dscr(n, [4, 128, S], BF16)
    for n in ("v", "ri", "kk"):
        dr[n] = dscr(n, [S, 512], BF16)
    dr["ghi"] = dscr("ghi", [S, 512], BF16)
    dr["glo"] = dscr("glo", [S, 512], BF16)
    dr["mixT"] = dscr("mixT", [1024, S], BF16)
    dr["tvec"] = dscr("tvec", [4, 384], BF16)
    dr["lbrep"] = dscr("lbrep", [128, DEPTH, 512], F32)
    dr["omlrep"] = dscr("omlrep", [128, DEPTH, 512], F32)
    setup(k, dr, ext)
    for l in range(nlayers):
        dr["x_in"] = ext["x"] if l == 0 else xres
        dr["x_out"] = xres
        phase_A(k, l, dr)
        phase_B(k, l, dr)
        with ExitStack() as wes:
            pre, thunks = alloc_D_weights(k, l, dr, wes)
            phase_C(k, l, dr, extra=thunks)
            phase_D(k, l, dr, final=(l == nlayers - 1), pre=pre)
    k.barrier()
    return nc


def make_in_maps(inputs):
    cf, cb16 = host_consts()
    f32 = lambda a: np.ascontiguousarray(np.asarray(a, dtype=np.float32))
    shared = {
        "w_in": f32(inputs["w_in"]), "w_out": f32(inputs["w_out"]),
        "w_up": f32(inputs["w_up"]), "w_down": f32(inputs["w_down"]),
        "lb_logits": f32(inputs["lb_logits"]), "lam_qk": f32(inputs["lam_qk"]),
        "rel_bias": f32(inputs["rel_bias"]), "final_g": f32(inputs["final_g"]),
        "g1": f32(np.asarray(inputs["norm1_g"]).reshape(DEPTH, 8, 128).transpose(2, 0, 1).reshape(128, 32)),
        "g2": f32(np.asarray(inputs["norm2_g"]).reshape(DEPTH, 8, 128).transpose(2, 0, 1).reshape(128, 32)),
        "gA": f32(np.asarray(inputs["attn_norm_g"]).T),
        "gH": f32(np.asarray(inputs["hgrn_norm_g"]).T),
        "cf": cf, "cb16": cb16,
    }
    x = f32(inputs["x"])
    return [dict(shared, x=x[b]) for b in range(x.shape[0])]


_NC_CACHE = {}


def kernel(**inputs):
    if "nc" not in _NC_CACHE:
        _NC_CACHE["nc"] = build_full()
    nc = _NC_CACHE["nc"]
    in_maps = make_in_maps(inputs)
    res = run_bass_kernel_spmd(nc, in_maps, core_ids=list(range(NCORES)))
    return np.stack([np.asarray(r["y"], dtype=np.float32) for r in res.results], axis=0)
```
